# Optimizing a Trainium2 kernel written in Bass

```python
import math
import jax, jax.numpy as jnp
from jax import lax
import numpy as np

D_MODEL = 1024
BATCH = 8
SEQ = 2048
DEPTH = 1

N_MEM = 256
DA_HEADS = 8
DA_QK_DIM = 64
DA_V_DIM = 2 * DA_QK_DIM
DA_QK_WIDTH = DA_HEADS * 2 * DA_QK_DIM
DA_V_WIDTH = DA_HEADS * DA_V_DIM
LRU_WIDTH = D_MODEL
LRU_BLOCKS = 8
LRU_BLOCK = LRU_WIDTH // LRU_BLOCKS
CONV_WIDTH = 4
LRU_C = 8.0
CA_HEADS = 4
CA_HEAD_DIM = D_MODEL // CA_HEADS
CA_WIDTH = CA_HEADS * CA_HEAD_DIM
D_FF = 2816
MACARON_WEIGHT = 0.5
N_BRANCH = 3
Q_BLOCK = 128
EPS = 1e-6
IN_WIDTHS = (DA_QK_WIDTH, DA_QK_WIDTH, DA_V_WIDTH, LRU_WIDTH, LRU_WIDTH, CA_WIDTH)
D_IN = sum(IN_WIDTHS)
IN_SPLITS = tuple(int(v) for v in np.cumsum(IN_WIDTHS)[:-1])

kernel_name = 'hybrid_diffattn_rglru_memxattn_macaron'


def rmsnorm(x, g):
    xf = x.astype(jnp.float32)
    y = xf * lax.rsqrt(jnp.mean(xf * xf, axis=-1, keepdims=True) + EPS)
    return (y * g.astype(jnp.float32)).astype(x.dtype)


def swiglu_half_step(x, pre_g, w_gate, w_up, w_down, post_g):
    h = rmsnorm(x, pre_g)
    f = (jax.nn.silu(h @ w_gate) * (h @ w_up)) @ w_down
    return x + MACARON_WEIGHT * rmsnorm(f, post_g)


def alibi_slopes(n_heads):
    return jnp.exp2(-8.0 * jnp.arange(1, n_heads + 1, dtype=jnp.float32) / n_heads)


def diff_attention(q, k, v, lam, lam_init, head_g):
    B, S = q.shape[0], q.shape[1]
    q = q.transpose(0, 2, 3, 1, 4)
    k = k.transpose(0, 2, 3, 1, 4)
    v = v.transpose(0, 2, 1, 3)
    scale = DA_QK_DIM ** -0.5
    slopes = alibi_slopes(DA_HEADS)[:, None, None, None]
    outs = []
    for start in range(0, S, Q_BLOCK):
        end = start + Q_BLOCK
        qb = q[:, :, :, start:end]
        kb = k[:, :, :, :end]
        vb = v[:, :, :end]
        s = jnp.einsum('bhmqd,bhmkd->bhmqk', qb, kb).astype(jnp.float32) * scale
        dist = (jnp.arange(start, end)[:, None] - jnp.arange(end)[None, :]).astype(jnp.float32)
        s = jnp.where(dist >= 0.0, s - slopes * dist, -jnp.inf)
        p = jax.nn.softmax(s, axis=-1)
        a = p[:, :, 0] - lam * p[:, :, 1]
        outs.append(jnp.einsum('bhqk,bhkd->bhqd', a.astype(vb.dtype), vb))
    o = jnp.concatenate(outs, axis=2)
    o = rmsnorm(o, head_g) * (1.0 - lam_init)
    return o.transpose(0, 2, 1, 3).reshape(B, S, DA_V_WIDTH)


def rg_lru_branch(xr, yr, conv_w, conv_b, w_a, b_a, w_x, b_x, lam):
    B, S, W = xr.shape
    xp = jnp.pad(xr, ((0, 0), (CONV_WIDTH - 1, 0), (0, 0)))
    xc = conv_b + sum(xp[:, t:t + S] * conv_w[t] for t in range(CONV_WIDTH))
    xb = xc.reshape(B, S, LRU_BLOCKS, LRU_BLOCK)
    r = jax.nn.sigmoid(jnp.einsum('bsnc,ncd->bsnd', xb, w_a).reshape(B, S, W) + b_a)
    i = jax.nn.sigmoid(jnp.einsum('bsnc,ncd->bsnd', xb, w_x).reshape(B, S, W) + b_x)
    log_a = -LRU_C * r.astype(jnp.float32) * jax.nn.softplus(-lam.astype(jnp.float32))
    a = jnp.exp(log_a)
    u = jnp.sqrt(-jnp.expm1(2.0 * log_a)) * (i * xc).astype(jnp.float32)

    def combine(left, right):
        a1, b1 = left
        a2, b2 = right
        return a1 * a2, a2 * b1 + b2

    _, h = lax.associative_scan(combine, (a, u), axis=1)
    return h.astype(xr.dtype) * jax.nn.gelu(yr)


def memory_cross_attention(qc, mem_n, w_mem_kv):
    B, S = qc.shape[0], qc.shape[1]
    M = mem_n.shape[1]
    kv = mem_n @ w_mem_kv
    km, vm = jnp.split(kv, 2, axis=-1)
    q = qc.reshape(B, S, CA_HEADS, CA_HEAD_DIM)
    km = km.reshape(B, M, CA_HEADS, CA_HEAD_DIM)
    vm = vm.reshape(B, M, CA_HEADS, CA_HEAD_DIM)
    s = jnp.einsum('bshd,bmhd->bhsm', q, km).astype(jnp.float32) * (CA_HEAD_DIM ** -0.5)
    p = jax.nn.softmax(s, axis=-1)
    o = jnp.einsum('bhsm,bmhd->bshd', p.astype(vm.dtype), vm)
    return o.reshape(B, S, CA_WIDTH)


def setup_inputs(seed: int = 0) -> dict:
    key = jax.random.key(seed)
    ks = iter(jax.random.split(key, 40))

    def nrm(shape, scale):
        return jax.random.normal(next(ks), shape, jnp.float32) * scale

    def gain(shape):
        return 1.0 + nrm(shape, 0.02)

    L, D = DEPTH, D_MODEL
    d = {}
    d['x'] = nrm((BATCH, SEQ, D), 1.0)
    d['mem'] = nrm((BATCH, N_MEM, D), 1.0)
    d['ffn1_pre_g'] = gain((L, D))
    d['ffn1_w_gate'] = nrm((L, D, D_FF), D ** -0.5)
    d['ffn1_w_up'] = nrm((L, D, D_FF), D ** -0.5)
    d['ffn1_w_down'] = nrm((L, D_FF, D), D_FF ** -0.5)
    d['ffn1_post_g'] = gain((L, D))
    d['mix_pre_g'] = gain((L, D))
    d['w_in'] = nrm((L, D, D_IN), D ** -0.5)
    d['da_lambda_q1'] = nrm((L, DA_QK_DIM), 0.1)
    d['da_lambda_k1'] = nrm((L, DA_QK_DIM), 0.1)
    d['da_lambda_q2'] = nrm((L, DA_QK_DIM), 0.1)
    d['da_lambda_k2'] = nrm((L, DA_QK_DIM), 0.1)
    d['da_head_g'] = gain((L, DA_V_DIM))
    d['w_da_out'] = nrm((L, DA_V_WIDTH, D), DA_V_WIDTH ** -0.5)
    d['lru_conv_w'] = nrm((L, CONV_WIDTH, LRU_WIDTH), CONV_WIDTH ** -0.5)
    d['lru_conv_b'] = nrm((L, LRU_WIDTH), 0.01)
    d['lru_w_a'] = nrm((L, LRU_BLOCKS, LRU_BLOCK, LRU_BLOCK), LRU_BLOCK ** -0.5)
    d['lru_b_a'] = nrm((L, LRU_WIDTH), 0.01)
    d['lru_w_x'] = nrm((L, LRU_BLOCKS, LRU_BLOCK, LRU_BLOCK), LRU_BLOCK ** -0.5)
    d['lru_b_x'] = nrm((L, LRU_WIDTH), 0.01)
    a_max = jax.random.uniform(next(ks), (L, LRU_WIDTH), jnp.float32, 0.9, 0.999)
    s_root = a_max ** (1.0 / LRU_C)
    d['lru_lambda'] = jnp.log(s_root) - jnp.log1p(-s_root)
    d['w_lru_out'] = nrm((L, LRU_WIDTH, D), LRU_WIDTH ** -0.5)
    d['mem_g'] = gain((L, D))
    d['w_mem_kv'] = nrm((L, D, 2 * CA_WIDTH), D ** -0.5)
    d['w_ca_out'] = nrm((L, CA_WIDTH, D), CA_WIDTH ** -0.5)
    d['w_branch_gate'] = nrm((L, D, N_BRANCH * D), D ** -0.5)
    d['b_branch_gate'] = nrm((L, N_BRANCH * D), 0.01)
    d['w_mix_out'] = nrm((L, D, D), D ** -0.5)
    d['mix_post_g'] = gain((L, D))
    d['ffn2_pre_g'] = gain((L, D))
    d['ffn2_w_gate'] = nrm((L, D, D_FF), D ** -0.5)
    d['ffn2_w_up'] = nrm((L, D, D_FF), D ** -0.5)
    d['ffn2_w_down'] = nrm((L, D_FF, D), D_FF ** -0.5)
    d['ffn2_post_g'] = gain((L, D))
    return d


def reference(x, mem, ffn1_pre_g, ffn1_w_gate, ffn1_w_up, ffn1_w_down, ffn1_post_g,
              mix_pre_g, w_in, da_lambda_q1, da_lambda_k1, da_lambda_q2, da_lambda_k2,
              da_head_g, w_da_out, lru_conv_w, lru_conv_b, lru_w_a, lru_b_a, lru_w_x,
              lru_b_x, lru_lambda, w_lru_out, mem_g, w_mem_kv, w_ca_out, w_branch_gate,
              b_branch_gate, w_mix_out, mix_post_g, ffn2_pre_g, ffn2_w_gate, ffn2_w_up,
              ffn2_w_down, ffn2_post_g):
    B, S, D = x.shape
    for l in range(DEPTH):
        x = swiglu_half_step(x, ffn1_pre_g[l], ffn1_w_gate[l], ffn1_w_up[l], ffn1_w_down[l], ffn1_post_g[l])

        h = rmsnorm(x, mix_pre_g[l])
        proj = h @ w_in[l]
        q_da, k_da, v_da, x_lru, y_lru, q_ca = jnp.split(proj, IN_SPLITS, axis=-1)

        lam_init = 0.8 - 0.6 * math.exp(-0.3 * l)
        lam = (jnp.exp(jnp.sum(da_lambda_q1[l].astype(jnp.float32) * da_lambda_k1[l].astype(jnp.float32)))
               - jnp.exp(jnp.sum(da_lambda_q2[l].astype(jnp.float32) * da_lambda_k2[l].astype(jnp.float32)))
               + lam_init)
        o_da = diff_attention(q_da.reshape(B, S, DA_HEADS, 2, DA_QK_DIM),
                              k_da.reshape(B, S, DA_HEADS, 2, DA_QK_DIM),
                              v_da.reshape(B, S, DA_HEADS, DA_V_DIM),
                              lam, lam_init, da_head_g[l])
        y_da = o_da @ w_da_out[l]

        o_lru = rg_lru_branch(x_lru, y_lru, lru_conv_w[l], lru_conv_b[l], lru_w_a[l], lru_b_a[l],
                              lru_w_x[l], lru_b_x[l], lru_lambda[l])
        y_lru_out = o_lru @ w_lru_out[l]

        o_ca = memory_cross_attention(q_ca, rmsnorm(mem, mem_g[l]), w_mem_kv[l])
        y_ca = o_ca @ w_ca_out[l]

        gates = jax.nn.sigmoid(h @ w_branch_gate[l] + b_branch_gate[l]).reshape(B, S, N_BRANCH, D)
        merged = gates[:, :, 0] * y_da + gates[:, :, 1] * y_lru_out + gates[:, :, 2] * y_ca
        x = x + rmsnorm(merged @ w_mix_out[l], mix_post_g[l])

        x = swiglu_half_step(x, ffn2_pre_g[l], ffn2_w_gate[l], ffn2_w_up[l], ffn2_w_down[l], ffn2_post_g[l])
    return x
```

```python
import contextlib
import os

import numpy as np
import ml_dtypes
import concourse.bass as bass
import concourse.mybir as mybir
from concourse.bass_utils import run_bass_kernel_spmd

F32 = mybir.dt.float32
BF16 = mybir.dt.bfloat16
AF = mybir.ActivationFunctionType
ALU = mybir.AluOpType

T = 2048
D = 1024
DC = 8
FF = 2816
FC = 22
NMEM = 256
EPS = 1e-6
NCORES = 8

PV = dict(g_f1pre=0, g_f1post=8, g_mixpre=16, g_mixpost=24, g_f2pre=32, g_f2post=40, g_mem=48,
          convw=56, convb=88, ba=96, bx=104, lam=112, bbg=120)
NV_IN = 144
PV_DER = dict(pg_f1=144, pg_f2=152, coef=160, tmp=168, hcoef=176, hba=184, hbx=192)
NV = 200


class Res:
    __slots__ = ("name", "w", "r")

    def __init__(self, name):
        self.name = name
        self.w = {}
        self.r = {}


class Eng:
    def __init__(self, name, sem, is_pe=False):
        self.name = name
        self.sem = sem
        self.cnt = 0
        self.waited = {}
        self.prog = []
        self.is_pe = is_pe


class Sched:
    def __init__(self, nc, stack):
        self.nc = nc
        self.stack = stack
        self.pe = Eng("pe", self._sem("s_pe"), is_pe=True)
        self.act = Eng("act", self._sem("s_act"))
        self.dve = Eng("dve", self._sem("s_dve"))
        self.pool = Eng("pool", self._sem("s_pool"))
        self.sp = Eng("sp", self._sem("s_sp"))
        self.engs = [self.pe, self.act, self.dve, self.pool, self.sp]
        self.dma_val = {}
        self.deferred = []

    def defer(self, thunk):
        self.deferred.append(thunk)

    def drain(self, k=None):
        n = len(self.deferred) if k is None else min(k, len(self.deferred))
        for _ in range(n):
            self.deferred.pop(0)()

    def _sem(self, name):
        return self.stack.enter_context(self.nc.semaphore(name))

    def new_dma_sem(self, name):
        s = self._sem(name)
        self.dma_val[s] = 0
        return s

    def _deps(self, eng, reads, writes):
        deps = {}

        def need(sem, val, raw):
            if sem is eng.sem and eng.is_pe:
                return
            if deps.get(sem, 0) < val:
                deps[sem] = val
        for r in reads:
            for s, v in r.w.items():
                need(s, v, True)
        for w in writes:
            for s, v in w.w.items():
                need(s, v, False)
            for s, v in w.r.items():
                need(s, v, False)
        pend = []
        for s, v in deps.items():
            if eng.waited.get(s, 0) < v:
                eng.waited[s] = v
                pend.append((s, v))
        for s, v in pend[:-1]:
            eng.prog.append(lambda e, s=s, v=v: e.wait_ge(s, v))
        return pend[-1] if pend else None

    def op(self, eng, fn, reads=(), writes=()):
        fw = self._deps(eng, reads, writes)
        eng.cnt += 1
        v = eng.cnt
        sem = eng.sem
        if fw is None:
            eng.prog.append(lambda e, fn=fn, sem=sem: fn(e).then_inc(sem, 1))
        else:
            eng.prog.append(lambda e, fn=fn, sem=sem, fw=fw: fn(e)._wait_ge(fw[0], fw[1]).then_inc(sem, 1))
        for r in reads:
            if r.r.get(sem, 0) < v:
                r.r[sem] = v
        for w in writes:
            w.w = {sem: v}
            w.r = {}

    def dma(self, eng, fns, sem, reads=(), writes=()):
        fw = self._deps(eng, reads, writes)
        if fw is not None:
            eng.prog.append(lambda e, fw=fw: e.wait_ge(fw[0], fw[1]))
        for fn in fns:
            self.dma_val[sem] += 16
            eng.prog.append(lambda e, fn=fn, sem=sem: fn(e).then_inc(sem, 16))
        v = self.dma_val[sem]
        for r in reads:
            if r.r.get(sem, 0) < v:
                r.r[sem] = v
        for w in writes:
            w.w = {sem: v}
            w.r = {}

    def barrier(self):
        for e in self.engs:
            for o in (self.pe, self.act, self.dve):
                if o is e and e.is_pe:
                    continue
                if o.cnt > 0 and e.waited.get(o.sem, 0) < o.cnt:
                    e.waited[o.sem] = o.cnt
                    e.prog.append(lambda q, s=o.sem, v=o.cnt: q.wait_ge(s, v))

    def emit(self):
        with self.nc.Block() as block:
            @block.tensor
            def _(e):
                for f in self.pe.prog:
                    f(e)

            @block.scalar
            def _(e):
                for f in self.act.prog:
                    f(e)

            @block.vector
            def _(e):
                for f in self.dve.prog:
                    f(e)

            @block.gpsimd
            def _(e):
                for f in self.pool.prog:
                    f(e)

            @block.sync
            def _(e):
                for f in self.sp.prog:
                    f(e)


class Arena:
    def __init__(self, t, size):
        self.t = t
        self.size = size
        self.off = 0

    def f32(self, n):
        a = self.t[:, self.off:self.off + n]
        self.off += n
        assert self.off <= self.size, ("arena overflow", self.off, self.size)
        return a

    def bf16(self, n):
        w = (n + 1) // 2
        a = self.t[:, self.off:self.off + w].bitcast(BF16)
        self.off += w
        assert self.off <= self.size, ("arena overflow", self.off, self.size)
        return a


class Slots:
    def __init__(self, items):
        self.items = items
        self.i = 0

    def next(self):
        it = self.items[self.i % len(self.items)]
        self.i += 1
        return it


def build_program(dbg=None):
    nc = bass.Bass("TRN2", target_bir_lowering=False)

    def din(name, shape, dt=F32):
        return nc.dram_tensor(name, list(shape), dt, kind="ExternalInput").ap()

    x_d = din("x", [T, D])
    mem_d = din("mem", [NMEM, D])
    out_d = nc.dram_tensor("out", [T, D], F32, kind="ExternalOutput").ap()
    Wd_ = {}
    for nm, shp in [("f1_wg", [D, FF]), ("f1_wu", [D, FF]), ("f1_wd", [FF, D]),
                    ("f2_wg", [D, FF]), ("f2_wu", [D, FF]), ("f2_wd", [FF, D]),
                    ("w_in", [D, 6144]), ("w_da_out", [D, D]), ("w_lru_out", [D, D]), ("w_ca_out", [D, D]),
                    ("w_bg", [D, 3 * D]), ("w_mix", [D, D]), ("w_mem_kv", [D, 2 * D]),
                    ("lru_wa", [D, 128]), ("lru_wx", [D, 128])]:
        Wd_[nm] = din(nm, shp)
    pvec_d = din("pvec", [128, NV_IN])
    lamv_d = din("lamv", [1, 256])
    headg_d = din("headg", [1, 128])
    c_identf_d = din("c_identf", [128, 128])
    c_bf_d = din("c_bf", [128, 4 * 128], BF16)
    qaug_d = din("qaug", [4, T], BF16)
    kaug_d = din("kaug", [32, T], BF16)

    ARENA_WORDS = 53200
    with contextlib.ExitStack() as st:
        S = Sched(nc, st)
        arena_t = st.enter_context(nc.sbuf_tensor("arena", [128, ARENA_WORDS], F32))
        A = Arena(arena_t, ARENA_WORDS)
        PSALL = st.enter_context(nc.psum_tensor("psall", [128, 8 * 512], F32))
        PS = [PSALL[:, i * 512:(i + 1) * 512] for i in range(8)]
        R_PS = [Res(f"ps{i}") for i in range(8)]

        PE, ACT, DVE, POOL, SP = S.pe, S.act, S.dve, S.pool, S.sp

        xT = A.f32(DC * T).rearrange("p (c t) -> p c t", c=DC)
        R_x = [Res(f"x{g}") for g in range(4)]
        identf = A.f32(128)
        cbf = A.bf16(4 * 128)
        identb, onesb, negmask, zerosb = cbf[:, 0:128], cbf[:, 128:256], cbf[:, 256:384], cbf[:, 384:512]
        pv = A.f32(NV)
        hg = A.f32(128)
        lamv = arena_t[:, ARENA_WORDS - 256:ARENA_WORDS]
        misc = A.f32(16)
        R_const = Res("const")
        R_pv = Res("pv")
        R_misc = Res("misc")
        epsc = misc[:, 0:1]
        neglam = misc[:, 1:2]
        qtr = misc[:, 4:5]
        PERSIST = A.off

        sem_c = S.new_dma_sem("d_const")
        S.dma(SP, [lambda e: e.dma_start(out=identf, in_=c_identf_d),
                   lambda e: e.dma_start(out=cbf, in_=c_bf_d)], sem_c, writes=[R_const])
        sem_pv = S.new_dma_sem("d_pv")
        S.dma(SP, [lambda e: e.dma_start(out=pv[:, 0:NV_IN], in_=pvec_d),
                   lambda e: e.dma_start(out=hg, in_=headg_d.partition_broadcast(128)),
                   lambda e: e.dma_start(out=lamv, in_=lamv_d.partition_broadcast(128))], sem_pv, writes=[R_pv])

        def pcol(name, c=0, n=1):
            o = (PV[name] if name in PV else PV_DER[name]) + c
            return pv[:, o:o + n]

        S.op(DVE, lambda e: e.memset(epsc, EPS), writes=[R_misc])
        S.op(DVE, lambda e: e.memset(qtr, 0.25), writes=[R_misc])
        S.op(DVE, lambda e: e.tensor_scalar(out=pcol("pg_f1", 0, 8), in0=pcol("g_f1post", 0, 8), scalar1=0.5,
                                            scalar2=None, op0=ALU.mult), reads=[R_pv], writes=[R_pv])
        S.op(DVE, lambda e: e.tensor_scalar(out=pcol("pg_f2", 0, 8), in0=pcol("g_f2post", 0, 8), scalar1=0.5,
                                            scalar2=None, op0=ALU.mult), reads=[R_pv], writes=[R_pv])
        S.op(DVE, lambda e: e.tensor_scalar(out=hg, in0=hg, scalar1=0.8, scalar2=None, op0=ALU.mult),
             reads=[R_pv], writes=[R_pv])
        S.op(ACT, lambda e: e.activation(out=pcol("tmp", 0, 8), in_=pcol("lam", 0, 8), func=AF.Exp, scale=-1.0),
             reads=[R_pv], writes=[R_pv])
        S.op(ACT, lambda e: e.activation(out=pcol("tmp", 0, 8), in_=pcol("tmp", 0, 8), func=AF.Ln, bias=1.0),
             reads=[R_pv], writes=[R_pv])
        S.op(DVE, lambda e: e.tensor_scalar(out=pcol("coef", 0, 8), in0=pcol("tmp", 0, 8), scalar1=-8.0,
                                            scalar2=None, op0=ALU.mult), reads=[R_pv], writes=[R_pv])
        for dst, src in (("hcoef", "coef"), ("hba", "ba"), ("hbx", "bx")):
            S.op(DVE, lambda e, dst=dst, src=src: e.tensor_scalar(out=pcol(dst, 0, 8), in0=pcol(src, 0, 8), scalar1=0.5,
                                                                 scalar2=None, op0=ALU.mult), reads=[R_pv], writes=[R_pv])
        S.op(DVE, lambda e: e.scalar_tensor_tensor(out=lamv[:, 0:64], in0=lamv[:, 0:64], scalar=1.0,
                                                   in1=lamv[:, 64:128], op0=ALU.mult, op1=ALU.mult,
                                                   accum_out=misc[:, 2:3]), reads=[R_pv], writes=[R_pv, R_misc])
        S.op(DVE, lambda e: e.scalar_tensor_tensor(out=lamv[:, 128:192], in0=lamv[:, 128:192], scalar=1.0,
                                                   in1=lamv[:, 192:256], op0=ALU.mult, op1=ALU.mult,
                                                   accum_out=misc[:, 3:4]), reads=[R_pv], writes=[R_pv, R_misc])
        S.op(ACT, lambda e: e.activation(out=misc[:, 2:4], in_=misc[:, 2:4], func=AF.Exp),
             reads=[R_misc], writes=[R_misc])
        S.op(DVE, lambda e: e.tensor_tensor(out=neglam, in0=misc[:, 3:4], in1=misc[:, 2:3], op=ALU.subtract),
             reads=[R_misc], writes=[R_misc])
        S.op(DVE, lambda e: e.tensor_scalar(out=neglam, in0=neglam, scalar1=-0.2, scalar2=None, op0=ALU.add),
             reads=[R_misc], writes=[R_misc])

        ring = [0]

        def nbank(lo=0, hi=6):
            b = lo + ring[0] % (hi - lo)
            ring[0] += 1
            return b

        flip = [0]

        def evac_eng():
            flip[0] += 1
            return ACT if flip[0] % 2 else DVE

        def copy_op(eng, out, in_, reads, writes, scale=None):
            if eng is ACT:
                if scale is None:
                    S.op(ACT, lambda e: e.activation(out=out, in_=in_, func=AF.Copy), reads=reads, writes=writes)
                else:
                    S.op(ACT, lambda e: e.activation(out=out, in_=in_, func=AF.Copy, scale=scale),
                         reads=reads, writes=writes)
            else:
                if scale is None:
                    S.op(DVE, lambda e: e.tensor_copy(out=out, in_=in_), reads=reads, writes=writes)
                else:
                    S.op(DVE, lambda e: e.tensor_scalar(out=out, in0=in_, scalar1=scale, scalar2=None,
                                                        op0=ALU.mult), reads=reads, writes=writes)

        def wload(dst, src, sem, R, max_last=None):
            if max_last is None:
                S.dma(POOL, [lambda e: e.dma_start(out=dst, in_=src)], sem, writes=[R])
            else:
                S.dma(POOL, [lambda e: e.dma_start(out=dst, in_=src, max_dma_last_dim=max_last)], sem, writes=[R])

        def rms_rstd(src_chunks, src_res, n, sq_slots, rstd_ap, R_rstd, bank, ndim=D):
            nchunk = len(src_chunks)
            for c, ap in enumerate(src_chunks):
                sq, R_sq = sq_slots.next()
                S.op(ACT, lambda e, sq=sq, ap=ap: e.activation(out=sq[:, 0:n], in_=ap, func=AF.Square),
                     reads=src_res, writes=[R_sq])
                S.op(PE, lambda e, sq=sq, c=c: e.matmul(PS[bank][:, 0:n], lhsT=onesb, rhs=sq[:, 0:n],
                                                        start=(c == 0), stop=(c == nchunk - 1)),
                     reads=[R_sq, R_const], writes=[R_PS[bank]])
            S.op(ACT, lambda e: e.activation(out=rstd_ap, in_=PS[bank][:, 0:n], func=AF.Sqrt,
                                             scale=1.0 / ndim, bias=epsc),
                 reads=[R_PS[bank], R_misc], writes=[R_rstd])
            S.op(DVE, lambda e: e.reciprocal(out=rstd_ap, in_=rstd_ap), reads=[R_rstd], writes=[R_rstd])

        A.off = PERSIST
        stg = []
        for i in range(8):
            stg.append((A.f32(D), Res(f"stg{i}"), S.new_dma_sem(f"d_stg{i}")))
        stg = Slots(stg)
        for i in range(16):
            buf, R_b, sem = stg.next()
            S.dma(SP if i % 2 == 0 else POOL,
                  [lambda e, buf=buf, i=i: e.dma_start(out=buf, in_=x_d[i * 128:(i + 1) * 128, :])],
                  sem, writes=[R_b])
            for hb in range(2):
                b = nbank()
                for j in range(4):
                    c = hb * 4 + j
                    S.op(PE, lambda e, b=b, j=j, c=c, buf=buf: e.transpose(
                        out=PS[b][:, j * 128:(j + 1) * 128], in_=buf[:, c * 128:(c + 1) * 128], identity=identf),
                        reads=[R_b, R_const], writes=[R_PS[b]])
                copy_op(evac_eng(), xT[:, hb * 4:(hb + 1) * 4, i * 128:(i + 1) * 128],
                        PS[b][:, :].rearrange("p (c t) -> p c t", c=4), [R_PS[b]], [R_x[i // 4]])
        S.barrier()

        def ffn_stage(pre, pgname, wg_d, wu_d, wd_d, tag, last=False):
            A.off = PERSIST
            hTh = A.bf16(DC * 1024).rearrange("p (c t) -> p c t", c=DC)
            R_h = [Res(f"{tag}h{g}") for g in range(2)]
            actT = A.bf16(FC * 1024).rearrange("p (c t) -> p c t", c=FC)
            R_a = [Res(f"{tag}a{g}") for g in range(2)]
            fT = A.f32(DC * 1024).rearrange("p (c t) -> p c t", c=DC)
            R_f = [Res(f"{tag}f{g}") for g in range(2)]
            wg_s = Slots([(A.bf16(DC * 256).rearrange("p (c n) -> p c n", c=DC), Res(f"{tag}wg{i}"),
                           S.new_dma_sem(f"d_{tag}wg{i}")) for i in range(2)])
            wu_s = Slots([(A.bf16(DC * 256).rearrange("p (c n) -> p c n", c=DC), Res(f"{tag}wu{i}"),
                           S.new_dma_sem(f"d_{tag}wu{i}")) for i in range(2)])
            wd_s = Slots([(A.bf16(FC * 128).rearrange("p (c n) -> p c n", c=FC), Res(f"{tag}wd{i}"),
                           S.new_dma_sem(f"d_{tag}wd{i}")) for i in range(2)])
            sq_s = Slots([(A.bf16(512), Res(f"{tag}sq{i}")) for i in range(4)])
            sg_s = Slots([(A.f32(512), Res(f"{tag}sg{i}")) for i in range(2)])
            rstd = A.f32(1024)
            R_rstd = [Res(f"{tag}rstd{g}") for g in range(2)]
            wg_v = wg_d.rearrange("(c p) n -> p c n", p=128)
            wu_v = wu_d.rearrange("(c p) n -> p c n", p=128)
            wd_v = wd_d.rearrange("(c p) n -> p c n", p=128)
            rstd_post = A.f32(1024)
            R_rpost = [Res(f"{tag}rpost{g}") for g in range(2)]

            def prenorm_rstd(half):
                t0 = half * 1024
                for g in range(2):
                    gg = half * 2 + g
                    tok = slice(t0 + g * 512, t0 + (g + 1) * 512)
                    rs = rstd[:, g * 512:(g + 1) * 512]
                    rms_rstd([xT[:, c, tok] for c in range(DC)], [R_x[gg]], 512, sq_s, rs, R_rstd[g], 6)

            def prenorm_apply(half):
                t0 = half * 1024
                for g in range(2):
                    gg = half * 2 + g
                    tok = slice(t0 + g * 512, t0 + (g + 1) * 512)
                    loc = slice(g * 512, (g + 1) * 512)
                    rs = rstd[:, loc]
                    for c in range(DC):
                        S.op(DVE, lambda e, c=c, tok=tok, loc=loc, rs=rs: e.scalar_tensor_tensor(
                            out=hTh[:, c, loc], in0=xT[:, c, tok], scalar=pcol(pre, c), in1=rs,
                            op0=ALU.mult, op1=ALU.mult), reads=[R_x[gg], R_pv, R_rstd[g]], writes=[R_h[g]])

            def gateup(half, mid_hook=None, drain=0):
                for gi in range(11):
                    wg, R_wg, s_wg = wg_s.next()
                    wu, R_wu, s_wu = wu_s.next()
                    wload(wg, wg_v[:, :, gi * 256:(gi + 1) * 256], s_wg, R_wg)
                    wload(wu, wu_v[:, :, gi * 256:(gi + 1) * 256], s_wu, R_wu)
                    if gi == 6 and mid_hook is not None:
                        mid_hook()
                    for cc in range(2):
                        c = gi * 2 + cc
                        for g in range(2):
                            loc = slice(g * 512, (g + 1) * 512)
                            bg, bu = nbank(), nbank()
                            for k in range(DC):
                                S.op(PE, lambda e, bg=bg, k=k, cc=cc, loc=loc, wg=wg: e.matmul(
                                    PS[bg][:, :], lhsT=wg[:, k, cc * 128:(cc + 1) * 128], rhs=hTh[:, k, loc],
                                    start=(k == 0), stop=(k == DC - 1)), reads=[R_wg, R_h[g]], writes=[R_PS[bg]])
                            for k in range(DC):
                                S.op(PE, lambda e, bu=bu, k=k, cc=cc, loc=loc, wu=wu: e.matmul(
                                    PS[bu][:, :], lhsT=wu[:, k, cc * 128:(cc + 1) * 128], rhs=hTh[:, k, loc],
                                    start=(k == 0), stop=(k == DC - 1)), reads=[R_wu, R_h[g]], writes=[R_PS[bu]])
                            sg, R_sg = sg_s.next()
                            S.op(ACT, lambda e, sg=sg, bg=bg: e.activation(out=sg, in_=PS[bg][:, :], func=AF.Silu),
                                 reads=[R_PS[bg]], writes=[R_sg])
                            S.op(DVE, lambda e, sg=sg, bu=bu, c=c, loc=loc: e.tensor_tensor(
                                out=actT[:, c, loc], in0=sg, in1=PS[bu][:, :], op=ALU.mult),
                                reads=[R_sg, R_PS[bu]], writes=[R_a[g]])
                            if drain:
                                S.drain(drain)

            def down(half, gouter=False):
                lag = []
                if gouter:
                    order = [(dc, g) for g in range(2) for dc in range(DC)]
                else:
                    order = [(dc, g) for dc in range(DC) for g in range(2)]
                wd = R_wd = None
                for (dc, g) in order:
                    if gouter or g == 0:
                        wd, R_wd, s_wd = wd_s.next()
                        wload(wd, wd_v[:, :, dc * 128:(dc + 1) * 128], s_wd, R_wd)
                    loc = slice(g * 512, (g + 1) * 512)
                    b = nbank()
                    while len(lag) > 1:
                        lag.pop(0)()
                    for c in range(FC):
                        S.op(PE, lambda e, b=b, c=c, loc=loc, wd=wd: e.matmul(
                            PS[b][:, :], lhsT=wd[:, c, :], rhs=actT[:, c, loc],
                            start=(c == 0), stop=(c == FC - 1)), reads=[R_wd, R_a[g]], writes=[R_PS[b]])
                    S.op(DVE, lambda e, b=b, dc=dc, loc=loc: e.tensor_copy(out=fT[:, dc, loc], in_=PS[b][:, :]),
                         reads=[R_PS[b]], writes=[R_f[g]])
                    sq, R_sq = sq_s.next()
                    S.op(ACT, lambda e, sq=sq, dc=dc, loc=loc: e.activation(out=sq, in_=fT[:, dc, loc],
                                                                          func=AF.Square),
                         reads=[R_f[g]], writes=[R_sq])
                    lag.append(lambda sq=sq, R_sq=R_sq, g=g, dc=dc: S.op(PE, lambda e: e.matmul(
                        PS[6 + g][:, :], lhsT=onesb, rhs=sq, start=(dc == 0), stop=(dc == DC - 1)),
                        reads=[R_sq, R_const], writes=[R_PS[6 + g]]))
                    if gouter:
                        S.drain(3)
                        if g == 0 and dc == DC - 1:
                            while lag:
                                lag.pop(0)()
                            for th in post_thunks(half, 0):
                                S.defer(th)
                while lag:
                    lag.pop(0)()

            def post_thunks(half, only_g=None):
                t0 = half * 1024
                th = []
                for g in ((0, 1) if only_g is None else (only_g,)):
                    gg = half * 2 + g
                    tok = slice(t0 + g * 512, t0 + (g + 1) * 512)
                    loc = slice(g * 512, (g + 1) * 512)
                    rs = rstd_post[:, loc]
                    th.append(lambda rs=rs, g=g: S.op(ACT, lambda e: e.activation(
                        out=rs, in_=PS[6 + g][:, :], func=AF.Sqrt, scale=1.0 / D, bias=epsc),
                        reads=[R_PS[6 + g], R_misc], writes=[R_rpost[g]]))
                    th.append(lambda rs=rs, g=g: S.op(DVE, lambda e: e.reciprocal(out=rs, in_=rs),
                                                      reads=[R_rpost[g]], writes=[R_rpost[g]]))
                    for dc in range(DC):
                        th.append(lambda dc=dc, loc=loc, rs=rs, g=g: S.op(DVE, lambda e: e.tensor_tensor(
                            out=fT[:, dc, loc], in0=fT[:, dc, loc], in1=rs, op=ALU.mult),
                            reads=[R_f[g], R_rpost[g]], writes=[R_f[g]]))
                        th.append(lambda dc=dc, loc=loc, tok=tok, g=g, gg=gg: S.op(DVE, lambda e: e.scalar_tensor_tensor(
                            out=xT[:, dc, tok], in0=fT[:, dc, loc], scalar=pcol(pgname, dc), in1=xT[:, dc, tok],
                            op0=ALU.mult, op1=ALU.add), reads=[R_f[g], R_pv, R_x[gg]], writes=[R_x[gg]]))
                return th

            prenorm_rstd(0)
            prenorm_apply(0)
            gateup(0, mid_hook=lambda: prenorm_rstd(1))
            prenorm_apply(1)
            down(0)
            for th in post_thunks(0):
                S.defer(th)
            gateup(1, drain=1)
            S.drain()
            down(1)
            for th in post_thunks(1):
                th()
            if not last:
                S.barrier()

        def mixer_stage(upto=None):
            A.off = PERSIST
            hT = A.bf16(DC * T).rearrange("p (c t) -> p c t", c=DC)
            R_h = [Res(f"mh{g}") for g in range(4)]
            O1BASE = A.off
            O1 = A.bf16(DC * T).rearrange("p (c t) -> p c t", c=DC)
            R_o1 = [Res(f"o1_{g}") for g in range(4)]
            YBASE = A.off
            O2 = A.bf16(DC * T).rearrange("p (c t) -> p c t", c=DC)
            R_o2 = [Res(f"o2_{g}") for g in range(4)]
            ZBASE = A.off
            w_in_v = Wd_["w_in"].rearrange("(c p) n -> p c n", p=128)
            w_bg_v = Wd_["w_bg"].rearrange("(c p) n -> p c n", p=128)

            sq_s = Slots([(A.bf16(512), Res(f"m0sq{i}")) for i in range(2)])
            rstd_s = Slots([(A.f32(512), Res(f"m0rs{i}")) for i in range(2)])
            for g in range(4):
                tok = slice(g * 512, (g + 1) * 512)
                rs, R_rs = rstd_s.next()
                rms_rstd([xT[:, c, tok] for c in range(DC)], [R_x[g]], 512, sq_s, rs, R_rs, 6 + g % 2)
                for c in range(DC):
                    S.op(DVE, lambda e, c=c, tok=tok, rs=rs: e.scalar_tensor_tensor(
                        out=hT[:, c, tok], in0=xT[:, c, tok], scalar=pcol("g_mixpre", c), in1=rs,
                        op0=ALU.mult, op1=ALU.mult), reads=[R_x[g], R_pv, R_rs], writes=[R_h[g]])
            S.barrier()

            A.off = YBASE
            qk = []
            for sset in range(2):
                d_ = {}
                for nm in ("QA", "QB", "KA", "KB"):
                    d_[nm] = A.bf16(T)
                    d_["R_" + nm] = Res(f"{nm}{sset}")
                d_["V1"] = A.bf16(16 * 130).rearrange("p (t n) -> p t n", t=16)
                d_["R_V"] = Res(f"V1_{sset}")
                d_["s_aug"] = S.new_dma_sem(f"d_kaug{sset}")
                qk.append(d_)
            e_s = Slots([(A.bf16(1024), Res(f"E{i}")) for i in range(3)])
            spair = [0]
            wq_s = Slots([(A.bf16(DC * 128).rearrange("p (c n) -> p c n", c=DC), Res(f"wqkv{i}"),
                           S.new_dma_sem(f"d_wqkv{i}")) for i in range(6)])
            o_all = A.f32(16 * 128).rearrange("p (t n) -> p t n", t=16)
            R_oall = Res("o_all")
            on_all = A.bf16(16 * 128).rearrange("p (t n) -> p t n", t=16)
            R_on = Res("on_all")
            oraw_s = Slots([(A.f32(4 * 130).rearrange("p (t m n) -> p t m n", t=2, m=2), Res(f"oraw{i}")) for i in range(2)])
            pending_T = []
            t1_s = Slots([(A.f32(128), Res(f"t1_{i}")) for i in range(2)])
            sm_s = Slots([(A.f32(4), Res(f"sm{i}")) for i in range(4)])
            ssq = A.f32(16)
            R_ssq = Res("ssq")
            junk = A.f32(128)
            R_junk = Res("junk")
            for qi_, d_ in enumerate(qk):
                sem_qaug = S.new_dma_sem(f"d_qaug{qi_}")
                for nm in ("QA", "QB", "KA", "KB"):
                    S.op(DVE, lambda e, ap=d_[nm]: e.memset(ap, 0.0), writes=[d_["R_" + nm]])
                S.op(DVE, lambda e, ap=d_["V1"]: e.memset(ap, 1.0), writes=[d_["R_V"]])
                S.dma(SP, [lambda e, d_=d_: e.dma_start(out=d_["QA"][64:68, :], in_=qaug_d),
                           lambda e, d_=d_: e.dma_start(out=d_["QB"][0:4, :], in_=qaug_d)], sem_qaug,
                      writes=[d_["R_QA"], d_["R_QB"]])
            def make_proj(h):
                d_ = qk[h % 2]
                QA, QB, KA, KB, V1 = d_["QA"], d_["QB"], d_["KA"], d_["KB"], d_["V1"]
                R_QA, R_QB, R_KA, R_KB, R_V = d_["R_QA"], d_["R_QB"], d_["R_KA"], d_["R_KB"], d_["R_V"]
                wq, R_wq, s_wq = wq_s.next()
                wk, R_wk, s_wk = wq_s.next()
                wv, R_wv, s_wv = wq_s.next()
                chunks = []

                def loads():
                    S.dma(SP, [lambda e: e.dma_start(out=KA[64:68, :], in_=kaug_d[h * 4:(h + 1) * 4, :]),
                               lambda e: e.dma_start(out=KB[0:4, :], in_=kaug_d[h * 4:(h + 1) * 4, :])],
                          d_["s_aug"], writes=[R_KA, R_KB])
                    wload(wq, w_in_v[:, :, h * 128:(h + 1) * 128], s_wq, R_wq)
                    wload(wk, w_in_v[:, :, 1024 + h * 128:1024 + (h + 1) * 128], s_wk, R_wk)
                    wload(wv, w_in_v[:, :, 2048 + h * 128:2048 + (h + 1) * 128], s_wv, R_wv)
                chunks.append(loads)
                for (w_, R_w, XA, XB, R_XA, R_XB, sc) in ((wq, R_wq, QA, QB, R_QA, R_QB, 0.125),
                                                          (wk, R_wk, KA, KB, R_KA, R_KB, None)):
                    for g in range(4):
                        def qk_chunk(w_=w_, R_w=R_w, XA=XA, XB=XB, R_XA=R_XA, R_XB=R_XB, sc=sc, g=g):
                            tok = slice(g * 512, (g + 1) * 512)
                            b = nbank(0, 6)
                            for k in range(DC):
                                S.op(PE, lambda e, b=b, k=k: e.matmul(
                                    PS[b][:, :], lhsT=w_[:, k, :], rhs=hT[:, k, tok], start=(k == 0), stop=(k == DC - 1)),
                                    reads=[R_w, R_h[g]], writes=[R_PS[b]])
                            copy_op(DVE if g % 2 else ACT, XA[0:64, tok], PS[b][0:64, :], [R_PS[b]], [R_XA], scale=sc)
                            copy_op(DVE, XB[64:128, tok], PS[b][64:128, :], [R_PS[b]], [R_XB], scale=sc)
                        chunks.append(qk_chunk)
                for tq in range(4):
                    def v_chunk(tq=tq):
                        bb = (nbank(0, 6), nbank(0, 6))
                        for k, t in [(k, t) for hp in range(2) for k in range(DC) for t in (2 * hp, 2 * hp + 1)]:
                            if True:
                                tl = tq * 4 + t
                                S.op(PE, lambda e, k=k, t=t, tl=tl: e.matmul(
                                    PS[bb[t % 2]][:, (t // 2) * 128:(t // 2) * 128 + 128],
                                    lhsT=hT[:, k, tl * 128:(tl + 1) * 128], rhs=wv[:, k, :],
                                    start=(k == 0), stop=(k == DC - 1)), reads=[R_wv, R_h[tq]], writes=[R_PS[bb[t % 2]]])
                        for par in range(2):
                            copy_op(DVE, V1[:, tq * 4 + par:tq * 4 + 4:2, 0:128],
                                    PS[bb[par]][:, 0:256].rearrange("p (t n) -> p t n", t=2), [R_PS[bb[par]]], [R_V])
                    chunks.append(v_chunk)
                return chunks

            for ch in make_proj(0):
                ch()
            for h in range(8):
                d_ = qk[h % 2]
                QA, QB, KA, KB, V1 = d_["QA"], d_["QB"], d_["KA"], d_["KB"], d_["V1"]
                R_QA, R_QB, R_KA, R_KB, R_V = d_["R_QA"], d_["R_QB"], d_["R_KA"], d_["R_KB"], d_["R_V"]
                units = []
                for p in range(8):
                    for m in range(2):
                        for g0 in range(0, p + 1, 2):
                            units.append((p, m, tuple(g for g in (g0, g0 + 1) if g <= p)))
                ust = {}
                oraw_of = {}

                def emit_S(u, KA=KA, KB=KB, QA=QA, QB=QB, R_KA=R_KA, R_KB=R_KB, R_QA=R_QA, R_QB=R_QB):
                    p, m, gs = u
                    i0, i1 = 2 * p, 2 * p + 1
                    Kx, Qx, R_Kx, R_Qx = (KA, QA, R_KA, R_QA) if m == 0 else (KB, QB, R_KB, R_QB)
                    base = 2 * (spair[0] % 3)
                    spair[0] += 1
                    allblocks = []
                    ntot = 0
                    per_g = []
                    for gi, g in enumerate(gs):
                        if g < p:
                            per_g.append([(gi, g, 2 * g, 0, 256), (gi, g, 2 * g + 1, 256, 256)])
                        else:
                            per_g.append([(gi, g, i0, 0, 256), (gi, g, i1, 256, 128)])
                        ntot = gi * 512 + (512 if g < p else 384)
                    order = []
                    for bi_ in range(2):
                        for lst in per_g:
                            order.append(lst[bi_])
                    for (gi, g, j, coff, ncol) in order:
                        sb_ = base + gi
                        qsl = slice(i0 * 128, i0 * 128 + 256) if ncol == 256 else slice(i1 * 128, i1 * 128 + 128)
                        diag = (g == p)
                        S.op(PE, lambda e, sb_=sb_, j=j, coff=coff, ncol=ncol, qsl=qsl, Kx=Kx, Qx=Qx, diag=diag: e.matmul(
                            PS[sb_][:, coff:coff + ncol], lhsT=Kx[:, j * 128:(j + 1) * 128], rhs=Qx[:, qsl],
                            start=True, stop=(not diag)), reads=[R_Kx, R_Qx], writes=[R_PS[sb_]])
                        if diag:
                            S.op(PE, lambda e, sb_=sb_, coff=coff: e.matmul(
                                PS[sb_][:, coff:coff + 128], lhsT=identb, rhs=negmask, start=False, stop=True),
                                reads=[R_const], writes=[R_PS[sb_]])
                        allblocks.append((j, gi * 512 + coff, ncol))
                    E, R_E = e_s.next()
                    S.op(ACT, lambda e, E=E, base=base, ntot=ntot: e.activation(
                        out=E[:, 0:ntot], in_=PSALL[:, base * 512:base * 512 + ntot], func=AF.Exp),
                        reads=[R_PS[base + gi] for gi in range(len(gs))], writes=[R_E])
                    ust[u] = (E, R_E, allblocks)

                def emit_PV(u, V1=V1, R_V=R_V, QA=QA, R_QA=R_QA):
                    p, m, gs = u
                    g = gs[-1]
                    E, R_E, blocks = ust.pop(u)
                    obm = 6 + m
                    if gs[0] == 0:
                        S.op(PE, lambda e, obm=obm: e.matmul(PS[obm][:, 0:512], lhsT=zerosb, rhs=QA[:, 0:512],
                                                            start=True, stop=False, skip_group_check=True),
                             reads=[R_const, R_QA], writes=[R_PS[obm]])
                    for (j, coff, ncol) in sorted(blocks):
                        tiles = [(0, coff), (1, coff + 128)] if ncol == 256 else [(1, coff)]
                        for (t, c0) in tiles:
                            last = 2 * p + t
                            S.op(PE, lambda e, obm=obm, t=t, c0=c0, j=j, E=E, last=last: e.matmul(
                                PS[obm][:, t * 256:t * 256 + 129], lhsT=E[:, c0:c0 + 128], rhs=V1[:, j, 0:129],
                                start=False, stop=(j == last), skip_group_check=True),
                                reads=[R_E, R_V], writes=[R_PS[obm]])
                    if g == p:
                        if p not in oraw_of:
                            oraw_of[p] = oraw_s.next()
                        opair, R_opair = oraw_of[p]
                        S.op(DVE, lambda e, opair=opair, obm=obm, m=m: e.tensor_copy(
                            out=opair[:, :, m, 0:129],
                            in_=PS[obm][:, :].rearrange("p (t n) -> p t n", t=2)[:, :, 0:129]),
                            reads=[R_PS[obm]], writes=[R_opair])
                        if m == 1:
                            for t in range(2):
                                i = 2 * p + t
                                oraw, R_oraw = opair[:, t], R_opair
                                sm, R_sm = sm_s.next()
                                t1, R_t1 = t1_s.next()
                                S.op(DVE, lambda e, sm=sm, oraw=oraw: e.reciprocal(out=sm[:, 0:2], in_=oraw[:, :, 128]),
                                     reads=[R_oraw], writes=[R_sm])
                                S.op(DVE, lambda e, sm=sm: e.tensor_scalar(out=sm[:, 2:3], in0=sm[:, 1:2], scalar1=neglam,
                                                                           scalar2=None, op0=ALU.mult),
                                     reads=[R_sm, R_misc], writes=[R_sm])
                                S.op(DVE, lambda e, sm=sm, t1=t1, oraw=oraw: e.tensor_scalar(
                                    out=t1, in0=oraw[:, 1, 0:128], scalar1=sm[:, 2:3], scalar2=None, op0=ALU.mult),
                                    reads=[R_sm, R_oraw], writes=[R_t1])
                                S.op(DVE, lambda e, sm=sm, t1=t1, oraw=oraw, i=i: e.scalar_tensor_tensor(
                                    out=o_all[:, i, :], in0=oraw[:, 0, 0:128], scalar=sm[:, 0:1], in1=t1,
                                    op0=ALU.mult, op1=ALU.add), reads=[R_sm, R_t1, R_oraw], writes=[R_oall])
                                S.op(DVE, lambda e, i=i: e.scalar_tensor_tensor(
                                    out=junk, in0=o_all[:, i, :], scalar=1.0, in1=o_all[:, i, :], op0=ALU.mult,
                                    op1=ALU.mult, accum_out=ssq[:, i:i + 1]), reads=[R_oall], writes=[R_junk, R_ssq])

                emit_S(units[0])
                emit_S(units[1])
                for n_ in range(len(units)):
                    if n_ + 2 < len(units):
                        emit_S(units[n_ + 2])
                    emit_PV(units[n_])
                    if n_ == 1 and pending_T:
                        pending_T.pop(0)()
                    if n_ == 2 and h + 1 < 8:
                        nxt = make_proj(h + 1)
                    if n_ >= 2 and n_ % 3 == 2 and h + 1 < 8 and nxt:
                        nxt.pop(0)()
                if h + 1 < 8:
                    while nxt:
                        nxt.pop(0)()
                S.op(ACT, lambda e: e.activation(out=ssq, in_=ssq, func=AF.Sqrt, scale=1.0 / 128, bias=epsc),
                     reads=[R_ssq, R_misc], writes=[R_ssq])
                S.op(DVE, lambda e: e.reciprocal(out=ssq, in_=ssq), reads=[R_ssq], writes=[R_ssq])
                S.op(DVE, lambda e: e.tensor_tensor(out=o_all, in0=o_all,
                                                    in1=ssq.unsqueeze(2).broadcast_to([128, 16, 128]), op=ALU.mult),
                     reads=[R_oall, R_ssq], writes=[R_oall])
                S.op(DVE, lambda e: e.tensor_tensor(out=on_all, in0=o_all,
                                                    in1=hg.unsqueeze(1).broadcast_to([128, 16, 128]), op=ALU.mult),
                     reads=[R_oall, R_pv], writes=[R_on])

                def do_T(h=h):
                    for half in range(2):
                        tb_ = nbank(0, 6)
                        psb = PS[tb_][:, :].bitcast(BF16)
                        for t in range(8):
                            tl = half * 8 + t
                            S.op(PE, lambda e, t=t, tl=tl, psb=psb: e.transpose(out=psb[:, t * 128:(t + 1) * 128],
                                                                             in_=on_all[:, tl, :], identity=identb),
                                 reads=[R_on, R_const], writes=[R_PS[tb_]])
                        copy_op(evac_eng(), O1[:, h, half * 1024:(half + 1) * 1024], psb, [R_PS[tb_]],
                                [R_o1[half * 2], R_o1[half * 2 + 1]])
                pending_T.append(do_T)
            while pending_T:
                pending_T.pop(0)()
            S.barrier()
            if upto == "da":
                return O1

            def merge_branch(bi, wout_d, Osrc, R_src, Mb, R_mb):
                wout_v = wout_d.rearrange("(c p) n -> p c n", p=128)
                sig_s = Slots([(A.f32(512), Res(f"sig{bi}_{i}")) for i in range(2)])
                t_s = Slots([(A.f32(512), Res(f"tt{bi}_{i}")) for i in range(2)])
                wo_s = Slots([(A.bf16(DC * 128).rearrange("p (c n) -> p c n", c=DC), Res(f"wo{bi}_{i}"),
                               S.new_dma_sem(f"d_wo{bi}_{i}")) for i in range(2)])
                wg_s = Slots([(A.bf16(DC * 128).rearrange("p (c n) -> p c n", c=DC), Res(f"wgt{bi}_{i}"),
                               S.new_dma_sem(f"d_wgt{bi}_{i}")) for i in range(2)])
                for dc in range(DC):
                    wo, R_wo, s1 = wo_s.next()
                    wgt, R_wgt, s2 = wg_s.next()
                    wload(wo, wout_v[:, :, dc * 128:(dc + 1) * 128], s1, R_wo)
                    wload(wgt, w_bg_v[:, :, bi * D + dc * 128:bi * D + (dc + 1) * 128], s2, R_wgt)
                    for g in range(4):
                        tok = slice(g * 512, (g + 1) * 512)
                        by, bgt = nbank(), nbank()
                        for c in range(DC):
                            S.op(PE, lambda e, by=by, c=c, tok=tok, wo=wo: e.matmul(
                                PS[by][:, :], lhsT=wo[:, c, :], rhs=Osrc[:, c, tok],
                                start=(c == 0), stop=(c == DC - 1)), reads=[R_wo, R_src[g]], writes=[R_PS[by]])
                        for c in range(DC):
                            S.op(PE, lambda e, bgt=bgt, c=c, tok=tok, wgt=wgt: e.matmul(
                                PS[bgt][:, :], lhsT=wgt[:, c, :], rhs=hT[:, c, tok],
                                start=(c == 0), stop=(c == DC - 1)), reads=[R_wgt, R_h[g]], writes=[R_PS[bgt]])
                        sg, R_sg = sig_s.next()
                        S.op(ACT, lambda e, sg=sg, bgt=bgt, dc=dc: e.activation(
                            out=sg, in_=PS[bgt][:, :], func=AF.Sigmoid, bias=pcol("bbg", bi * 8 + dc)),
                            reads=[R_PS[bgt], R_pv], writes=[R_sg])
                        if bi == 0:
                            S.op(DVE, lambda e, sg=sg, by=by, dc=dc, tok=tok: e.tensor_tensor(
                                out=Mb[:, dc, tok], in0=sg, in1=PS[by][:, :], op=ALU.mult),
                                reads=[R_sg, R_PS[by]], writes=[R_mb[g]])
                        else:
                            tt, R_tt = t_s.next()
                            S.op(DVE, lambda e, sg=sg, by=by, tt=tt: e.tensor_tensor(
                                out=tt, in0=sg, in1=PS[by][:, :], op=ALU.mult),
                                reads=[R_sg, R_PS[by]], writes=[R_tt])
                            S.op(DVE, lambda e, tt=tt, dc=dc, tok=tok: e.tensor_tensor(
                                out=Mb[:, dc, tok], in0=Mb[:, dc, tok], in1=tt, op=ALU.add),
                                reads=[R_tt, R_mb[g]], writes=[R_mb[g]])
                S.barrier()

            A.off = ZBASE
            M_, R_m = O2, R_o2
            merge_branch(0, Wd_["w_da_out"], O1, R_o1, M_, R_m)
            O2, R_o2 = O1, R_o1
            if upto == "m0":
                return M_

            A.off = ZBASE
            L = 1024
            xl = A.f32(4 + L); R_xl = Res("xl")
            xc_s = [(A.f32(L), Res(f"xc{i}")) for i in range(2)]
            tr_s = [(A.f32(L), Res(f"tr{i}")) for i in range(2)]
            ti_s = [(A.f32(L), Res(f"ti{i}")) for i in range(2)]
            sb2 = A.f32(L); R_sb2 = Res("sb2")
            yg_s = [(A.bf16(L), Res(f"yg{i}")) for i in range(2)]
            hlast = A.f32(2); R_hl = Res("hlast")
            wxy_s = Slots([(A.bf16(DC * 128).rearrange("p (c n) -> p c n", c=DC), Res(f"wxy{i}"),
                            S.new_dma_sem(f"d_wxy{i}")) for i in range(3)])
            wax_s = Slots([(A.f32(256), Res(f"wax{i}"), S.new_dma_sem(f"d_wax{i}")) for i in range(2)])
            wcur = {}

            def lru_round(nb_, nf_):
                B = nf_ is not None
                A_ = nb_ is not None
                if B:
                    c, half = divmod(nf_, 2)
                    t0 = half * L
                    if half == 0:
                        wx, R_wx, s_wx = wxy_s.next()
                        wy, R_wy, s_wy = wxy_s.next()
                        wax, R_wax, s_wax = wax_s.next()
                        wload(wx, w_in_v[:, :, 3072 + c * 128:3072 + (c + 1) * 128], s_wx, R_wx)
                        wload(wy, w_in_v[:, :, 4096 + c * 128:4096 + (c + 1) * 128], s_wy, R_wy)
                        S.dma(SP, [lambda e, c=c, wax=wax: e.dma_start(out=wax[:, 0:128],
                                                                       in_=Wd_["lru_wa"][c * 128:(c + 1) * 128, :]),
                                   lambda e, c=c, wax=wax: e.dma_start(out=wax[:, 128:256],
                                                                       in_=Wd_["lru_wx"][c * 128:(c + 1) * 128, :])],
                              s_wax, writes=[R_wax])
                        wcur[c] = (wx, R_wx, wy, R_wy, wax, R_wax)
                    wx, R_wx, wy, R_wy, wax, R_wax = wcur[c]
                    xc, R_xc = xc_s[nf_ % 2]
                    tr, R_tr = tr_s[nf_ % 2]
                    ti, R_ti = ti_s[nf_ % 2]
                    yg, R_yg = yg_s[nf_ % 2]
                    ybanks = (4, 5) if nf_ % 2 == 0 else (6, 7)
                    if half == 0:
                        S.op(DVE, lambda e: e.memset(xl[:, 0:4], 0.0), writes=[R_xl])
                    else:
                        S.op(DVE, lambda e: e.tensor_copy(out=xl[:, 1:4], in_=xl[:, L + 1:L + 4]),
                             reads=[R_xl], writes=[R_xl])
                    for g in range(2):
                        tok = slice(t0 + g * 512, t0 + (g + 1) * 512)
                        gq = half * 2 + g
                        bx_ = nbank(0, 4)
                        for k in range(DC):
                            S.op(PE, lambda e, bx_=bx_, k=k, tok=tok, wx=wx: e.matmul(
                                PS[bx_][:, :], lhsT=wx[:, k, :], rhs=hT[:, k, tok], start=(k == 0), stop=(k == DC - 1)),
                                reads=[R_wx, R_h[gq]], writes=[R_PS[bx_]])
                        S.op(ACT, lambda e, bx_=bx_, g=g: e.activation(out=xl[:, 4 + g * 512:4 + (g + 1) * 512],
                                                                       in_=PS[bx_][:, :], func=AF.Copy),
                             reads=[R_PS[bx_]], writes=[R_xl])
                    for g in range(2):
                        tok = slice(t0 + g * 512, t0 + (g + 1) * 512)
                        gq = half * 2 + g
                        by_ = ybanks[g]
                        for k in range(DC):
                            S.op(PE, lambda e, by_=by_, k=k, tok=tok, wy=wy: e.matmul(
                                PS[by_][:, :], lhsT=wy[:, k, :], rhs=hT[:, k, tok], start=(k == 0), stop=(k == DC - 1)),
                                reads=[R_wy, R_h[gq]], writes=[R_PS[by_]])
                if A_:
                    ca, halfa = divmod(nb_, 2)
                    ta0 = halfa * L
                    xca, R_xca = xc_s[nb_ % 2]
                    tra, R_tra = tr_s[nb_ % 2]
                    tia, R_tia = ti_s[nb_ % 2]
                    yga, R_yga = yg_s[nb_ % 2]
                    S.op(DVE, lambda e: e.scalar_tensor_tensor(out=tia, in0=tia, scalar=1.0, in1=xca,
                                                               op0=ALU.add, op1=ALU.mult),
                         reads=[R_tia, R_xca], writes=[R_tia])
                    S.op(ACT, lambda e: e.activation(out=sb2, in_=tra, func=AF.Exp, scale=pcol("coef", ca),
                                                     bias=pcol("coef", ca)),
                         reads=[R_tra, R_pv], writes=[R_sb2])
                    S.op(ACT, lambda e: e.activation(out=tra, in_=tra, func=AF.Exp, scale=pcol("hcoef", ca),
                                                     bias=pcol("hcoef", ca)),
                         reads=[R_tra, R_pv], writes=[R_tra])
                if B:
                    S.op(DVE, lambda e: e.tensor_scalar(out=xc, in0=xl[:, 4:4 + L], scalar1=pcol("convw", 24 + c),
                                                        scalar2=pcol("convb", c), op0=ALU.mult, op1=ALU.add),
                         reads=[R_xl, R_pv], writes=[R_xc])
                    for tap in (2, 1, 0):
                        sh = 3 - tap
                        S.op(DVE, lambda e, tap=tap, sh=sh: e.scalar_tensor_tensor(
                            out=xc, in0=xl[:, 4 - sh:4 - sh + L], scalar=pcol("convw", tap * 8 + c), in1=xc,
                            op0=ALU.mult, op1=ALU.add), reads=[R_xl, R_pv, R_xc], writes=[R_xc])
                if A_:
                    S.op(ACT, lambda e: e.activation(out=sb2, in_=sb2, func=AF.Sqrt, scale=-0.25, bias=qtr),
                         reads=[R_sb2, R_misc], writes=[R_sb2])
                if B:
                    rib = []
                    for g in range(2):
                        loc = slice(g * 512, (g + 1) * 512)
                        br_, bi_ = nbank(0, 4), nbank(0, 4)
                        S.op(PE, lambda e, br_=br_, loc=loc: e.matmul(
                            PS[br_][:, :], lhsT=wax[:, 0:128], rhs=xc[:, loc], start=True, stop=True),
                            reads=[R_wax, R_xc], writes=[R_PS[br_]])
                        S.op(PE, lambda e, bi_=bi_, loc=loc: e.matmul(
                            PS[bi_][:, :], lhsT=wax[:, 128:256], rhs=xc[:, loc], start=True, stop=True),
                            reads=[R_wax, R_xc], writes=[R_PS[bi_]])
                        rib.append((br_, bi_, loc))
                if A_:
                    S.op(DVE, lambda e: e.tensor_tensor(out=tia, in0=tia, in1=sb2, op=ALU.mult),
                         reads=[R_tia, R_sb2], writes=[R_tia])
                    if halfa == 0:
                        S.op(DVE, lambda e: e.tensor_tensor_scan(out=tia, data0=tra, data1=tia, initial=0.0,
                                                                 op0=ALU.mult, op1=ALU.add),
                             reads=[R_tra, R_tia], writes=[R_tia])
                    else:
                        S.op(DVE, lambda e: e.tensor_tensor_scan(out=tia, data0=tra, data1=tia, initial=hlast[:, 0:1],
                                                                 op0=ALU.mult, op1=ALU.add),
                             reads=[R_tra, R_tia, R_hl], writes=[R_tia])
                    S.op(DVE, lambda e: e.tensor_copy(out=hlast[:, 0:1], in_=tia[:, L - 1:L]),
                         reads=[R_tia], writes=[R_hl])
                    S.op(DVE, lambda e: e.tensor_tensor(out=O2[:, ca, ta0:ta0 + L], in0=tia, in1=yga, op=ALU.mult),
                         reads=[R_tia, R_yga], writes=[R_o2[halfa * 2], R_o2[halfa * 2 + 1]])
                if B:
                    for (br_, bi_, loc) in rib:
                        S.op(ACT, lambda e, br_=br_, loc=loc: e.activation(
                            out=tr[:, loc], in_=PS[br_][:, :], func=AF.Tanh, scale=0.5, bias=pcol("hba", c)),
                            reads=[R_PS[br_], R_pv], writes=[R_tr])
                        S.op(ACT, lambda e, bi_=bi_, loc=loc: e.activation(
                            out=ti[:, loc], in_=PS[bi_][:, :], func=AF.Tanh, scale=0.5, bias=pcol("hbx", c)),
                            reads=[R_PS[bi_], R_pv], writes=[R_ti])
                    for g in range(2):
                        loc = slice(g * 512, (g + 1) * 512)
                        S.op(ACT, lambda e, g=g, loc=loc: e.activation(
                            out=yg[:, loc], in_=PS[ybanks[g]][:, :], func=AF.Gelu_apprx_tanh),
                            reads=[R_PS[ybanks[g]]], writes=[R_yg])

            lru_round(None, 0)
            for n in range(16):
                lru_round(n, n + 1 if n + 1 < 16 else None)
            S.barrier()
            if upto == "lru":
                return O2
            A.off = ZBASE
            merge_branch(1, Wd_["w_lru_out"], O2, R_o2, M_, R_m)
            if upto == "m1":
                return M_

            A.off = ZBASE
            memT = A.bf16(DC * NMEM).rearrange("p (c t) -> p c t", c=DC); R_memT = Res("memT")
            ZM = A.off
            mstg = A.f32(D); R_mstg = Res("mstg"); s_mstg = S.new_dma_sem("d_mstg")
            memTf = A.f32(DC * NMEM).rearrange("p (c t) -> p c t", c=DC); R_memTf = Res("memTf")
            sq_s = Slots([(A.bf16(512), Res(f"m3sq{i}")) for i in range(2)])
            rsm = A.f32(NMEM); R_rsm = Res("rsm")
            for i in range(2):
                S.dma(SP, [lambda e, i=i: e.dma_start(out=mstg, in_=mem_d[i * 128:(i + 1) * 128, :])], s_mstg,
                      writes=[R_mstg])
                for hb_ in range(2):
                    b = nbank()
                    for j in range(4):
                        c = hb_ * 4 + j
                        S.op(PE, lambda e, b=b, j=j, c=c: e.transpose(out=PS[b][:, j * 128:(j + 1) * 128],
                                                                    in_=mstg[:, c * 128:(c + 1) * 128], identity=identf),
                             reads=[R_mstg, R_const], writes=[R_PS[b]])
                    copy_op(evac_eng(), memTf[:, hb_ * 4:(hb_ + 1) * 4, i * 128:(i + 1) * 128],
                            PS[b][:, :].rearrange("p (c t) -> p c t", c=4), [R_PS[b]], [R_memTf])
            rms_rstd([memTf[:, c, :] for c in range(DC)], [R_memTf], NMEM, sq_s, rsm, R_rsm, 6)
            for c in range(DC):
                S.op(DVE, lambda e, c=c: e.scalar_tensor_tensor(out=memT[:, c, :], in0=memTf[:, c, :],
                                                                scalar=pcol("g_mem", c), in1=rsm,
                                                                op0=ALU.mult, op1=ALU.mult),
                     reads=[R_memTf, R_pv, R_rsm], writes=[R_memT])
            S.barrier()
            A.off = ZM
            KmT = A.bf16(DC * NMEM).rearrange("p (c t) -> p c t", c=DC); R_KmT = Res("KmT")
            Vm = A.bf16(2 * D).rearrange("p (m n) -> p m n", m=2); R_Vm = Res("Vm")
            wkv_s = Slots([(A.bf16(DC * 256).rearrange("p (c n) -> p c n", c=DC), Res(f"wkv{i}"),
                            S.new_dma_sem(f"d_wkv{i}")) for i in range(2)])
            qc_s = Slots([(A.bf16(2 * 512).rearrange("p (d t) -> p d t", d=2), Res(f"qc{i}")) for i in range(3)])
            et_s = Slots([(A.bf16(2 * 512).rearrange("p (m t) -> p m t", m=2), Res(f"et{i}")) for i in range(2)])
            rd_s = Slots([(A.f32(512), Res(f"rd{i}")) for i in range(2)])
            w_kv_v = Wd_["w_mem_kv"].rearrange("(c p) n -> p c n", p=128)
            for cg in range(4):
                wkv, R_wkv, s_wkv = wkv_s.next()
                wload(wkv, w_kv_v[:, :, cg * 256:(cg + 1) * 256], s_wkv, R_wkv)
                bb = (nbank(), nbank())
                for k in range(DC):
                    for cc in range(2):
                        S.op(PE, lambda e, b=bb[cc], k=k, cc=cc, wkv=wkv: e.matmul(
                            PS[b][:, 0:NMEM], lhsT=wkv[:, k, cc * 128:(cc + 1) * 128], rhs=memT[:, k, :],
                            start=(k == 0), stop=(k == DC - 1)), reads=[R_wkv, R_memT], writes=[R_PS[bb[cc]]])
                for cc in range(2):
                    copy_op(evac_eng(), KmT[:, cg * 2 + cc, :], PS[bb[cc]][:, 0:NMEM], [R_PS[bb[cc]]], [R_KmT],
                            scale=1.0 / 16)
            for cg in range(4):
                wkv, R_wkv, s_wkv = wkv_s.next()
                wload(wkv, w_kv_v[:, :, D + cg * 256:D + (cg + 1) * 256], s_wkv, R_wkv)
                bb = (nbank(), nbank())
                for k in range(DC):
                    for mc in range(2):
                        S.op(PE, lambda e, b=bb[mc], k=k, mc=mc, wkv=wkv: e.matmul(
                            PS[b][:, 0:256], lhsT=memT[:, k, mc * 128:(mc + 1) * 128], rhs=wkv[:, k, :],
                            start=(k == 0), stop=(k == DC - 1)), reads=[R_wkv, R_memT], writes=[R_PS[bb[mc]]])
                for mc in range(2):
                    copy_op(evac_eng(), Vm[:, mc, cg * 256:(cg + 1) * 256], PS[bb[mc]][:, 0:256], [R_PS[bb[mc]]], [R_Vm])
            cst = {}
            wq_of = {}

            def ca_P(u):
                hh, g = u
                if g == 0:
                    wqc, R_wqc, s_wqc = wkv_s.next()
                    wload(wqc, w_in_v[:, :, 5120 + hh * 256:5120 + (hh + 1) * 256], s_wqc, R_wqc)
                    wq_of[hh] = (wqc, R_wqc)
                wqc, R_wqc = wq_of[hh]
                tok = slice(g * 512, (g + 1) * 512)
                qc, R_qc = qc_s.next()
                for dd in range(2):
                    b = nbank()
                    for k in range(DC):
                        S.op(PE, lambda e, b=b, k=k, dd=dd, tok=tok, wqc=wqc: e.matmul(
                            PS[b][:, :], lhsT=wqc[:, k, dd * 128:(dd + 1) * 128], rhs=hT[:, k, tok],
                            start=(k == 0), stop=(k == DC - 1)), reads=[R_wqc, R_h[g]], writes=[R_PS[b]])
                    copy_op(DVE if dd else ACT, qc[:, dd, :], PS[b][:, :], [R_PS[b]], [R_qc])
                cst[u] = [qc, R_qc]

            def ca_S(u):
                hh, g = u
                qc, R_qc = cst[u]
                et, R_et = et_s.next()
                for mc in range(2):
                    b = nbank()
                    for dd in range(2):
                        S.op(PE, lambda e, b=b, dd=dd, mc=mc, hh=hh, qc=qc: e.matmul(
                            PS[b][:, :], lhsT=KmT[:, hh * 2 + dd, mc * 128:(mc + 1) * 128], rhs=qc[:, dd, :],
                            start=(dd == 0), stop=(dd == 1)), reads=[R_KmT, R_qc], writes=[R_PS[b]])
                    S.op(ACT, lambda e, b=b, mc=mc, et=et: e.activation(out=et[:, mc, :], in_=PS[b][:, :], func=AF.Exp),
                         reads=[R_PS[b]], writes=[R_et])
                cst[u] = [et, R_et]

            def ca_V(u):
                hh, g = u
                tok = slice(g * 512, (g + 1) * 512)
                et, R_et = cst.pop(u)
                bd = nbank()
                for mc in range(2):
                    S.op(PE, lambda e, bd=bd, mc=mc, et=et: e.matmul(PS[bd][:, :], lhsT=onesb, rhs=et[:, mc, :],
                                                                   start=(mc == 0), stop=(mc == 1)),
                         reads=[R_et, R_const], writes=[R_PS[bd]])
                rd, R_rd = rd_s.next()
                S.op(DVE, lambda e, bd=bd, rd=rd: e.reciprocal(out=rd, in_=PS[bd][:, :]), reads=[R_PS[bd]],
                     writes=[R_rd])
                for dv in range(2):
                    b = nbank()
                    for mc in range(2):
                        S.op(PE, lambda e, b=b, mc=mc, dv=dv, hh=hh, et=et: e.matmul(
                            PS[b][:, :], lhsT=Vm[:, mc, hh * 256 + dv * 128:hh * 256 + (dv + 1) * 128],
                            rhs=et[:, mc, :], start=(mc == 0), stop=(mc == 1)),
                            reads=[R_Vm, R_et], writes=[R_PS[b]])
                    S.op(DVE, lambda e, b=b, rd=rd, hh=hh, dv=dv, tok=tok: e.tensor_tensor(
                        out=O2[:, hh * 2 + dv, tok], in0=PS[b][:, :], in1=rd, op=ALU.mult),
                        reads=[R_PS[b], R_rd], writes=[R_o2[g]])

            cunits = [(hh, g) for hh in range(4) for g in range(4)]
            ca_P(cunits[0])
            ca_P(cunits[1])
            ca_S(cunits[0])
            for n_ in range(16):
                if n_ + 2 < 16:
                    ca_P(cunits[n_ + 2])
                if n_ + 1 < 16:
                    ca_S(cunits[n_ + 1])
                ca_V(cunits[n_])
            S.barrier()
            if upto == "ca":
                return O2
            A.off = ARENA_WORDS - DC * D // 2
            wm = A.bf16(DC * D).rearrange("p (c n) -> p c n", c=DC); R_wm = Res("wm"); s_wm = S.new_dma_sem("d_wm")
            wload(wm, Wd_["w_mix"].rearrange("(c p) n -> p c n", p=128), s_wm, R_wm)
            A.off = ZBASE
            merge_branch(2, Wd_["w_ca_out"], O2, R_o2, M_, R_m)
            if upto == "m2":
                return M_

            A.off = O1BASE
            f_s = Slots([(A.f32(DC * 512).rearrange("p (c t) -> p c t", c=DC), Res(f"m4f{i}"), A.f32(512),
                          Res(f"rs4_{i}")) for i in range(1)])
            YTOP = ZBASE
            f_s2 = (arena_t[:, YTOP:YTOP + DC * 512].rearrange("p (c t) -> p c t", c=DC), Res("m4f1"),
                    arena_t[:, YTOP + DC * 512:YTOP + DC * 512 + 512], Res("rs4_1"))
            f_list = [f_s.items[0], f_s2]
            sq_s = Slots([(A.bf16(512), Res(f"m4sq{i}")) for i in range(4)])
            lag = []
            for g in range(4):
                tok = slice(g * 512, (g + 1) * 512)
                fT, R_f, rs4, R_rs4 = f_list[g % 2]
                sb_ = 6 + g % 2
                for dc in range(DC):
                    b = nbank()
                    while len(lag) > 1:
                        lag.pop(0)()
                    for c in range(DC):
                        S.op(PE, lambda e, b=b, c=c, dc=dc, tok=tok: e.matmul(
                            PS[b][:, :], lhsT=wm[:, c, dc * 128:(dc + 1) * 128], rhs=M_[:, c, tok],
                            start=(c == 0), stop=(c == DC - 1)), reads=[R_wm, R_m[g]], writes=[R_PS[b]])
                    sq, R_sq = sq_s.next()
                    S.op(ACT, lambda e, sq=sq, b=b: e.activation(out=sq, in_=PS[b][:, :], func=AF.Square),
                         reads=[R_PS[b]], writes=[R_sq])
                    S.op(ACT, lambda e, b=b, dc=dc, fT=fT: e.activation(out=fT[:, dc, :], in_=PS[b][:, :], func=AF.Copy),
                         reads=[R_PS[b]], writes=[R_f])
                    lag.append(lambda sq=sq, R_sq=R_sq, dc=dc, sb_=sb_: S.op(PE, lambda e: e.matmul(
                        PS[sb_][:, :], lhsT=onesb, rhs=sq, start=(dc == 0), stop=(dc == DC - 1)),
                        reads=[R_sq, R_const], writes=[R_PS[sb_]]))
                    S.drain(3)
                while lag:
                    lag.pop(0)()
                S.defer(lambda rs4=rs4, R_rs4=R_rs4, sb_=sb_: S.op(ACT, lambda e: e.activation(
                    out=rs4, in_=PS[sb_][:, :], func=AF.Sqrt, scale=1.0 / D, bias=epsc),
                    reads=[R_PS[sb_], R_misc], writes=[R_rs4]))
                S.defer(lambda rs4=rs4, R_rs4=R_rs4: S.op(DVE, lambda e: e.reciprocal(out=rs4, in_=rs4),
                                                          reads=[R_rs4], writes=[R_rs4]))
                for dc in range(DC):
                    S.defer(lambda dc=dc, fT=fT, rs4=rs4, R_f=R_f, R_rs4=R_rs4: S.op(DVE, lambda e: e.tensor_tensor(
                        out=fT[:, dc, :], in0=fT[:, dc, :], in1=rs4, op=ALU.mult), reads=[R_f, R_rs4], writes=[R_f]))
                    S.defer(lambda dc=dc, fT=fT, tok=tok, R_f=R_f, g=g: S.op(DVE, lambda e: e.scalar_tensor_tensor(
                        out=xT[:, dc, tok], in0=fT[:, dc, :], scalar=pcol("g_mixpost", dc), in1=xT[:, dc, tok],
                        op0=ALU.mult, op1=ALU.add), reads=[R_f, R_pv, R_x[g]], writes=[R_x[g]]))
            S.drain()
            S.barrier()
            return None

        stages = os.environ.get("MK_STAGES", "all") if dbg is None else dbg
        dump = None
        if stages != "x":
            ffn_stage("g_f1pre", "pg_f1", Wd_["f1_wg"], Wd_["f1_wu"], Wd_["f1_wd"], "f1")
        if stages not in ("x", "f1"):
            dump = mixer_stage(None if stages in ("all", "mix") else stages)
        if stages == "all":
            ffn_stage("g_f2pre", "pg_f2", Wd_["f2_wg"], Wd_["f2_wu"], Wd_["f2_wd"], "f2", last=True)
        if dump is not None:
            for g in range(4):
                tok = slice(g * 512, (g + 1) * 512)
                S.op(DVE, lambda e, tok=tok: e.tensor_copy(out=xT[:, :, tok], in_=dump[:, :, tok]), writes=[R_x[g]])
            S.barrier()

        if stages == "all":
            A.off = PERSIST
        else:
            A.off = PERSIST
        ostg = Slots([(A.f32(D), Res(f"ostg{i}"), S.new_dma_sem(f"d_ostg{i}")) for i in range(4)])
        out_sems = []
        for i in range(16):
            buf, R_b, sem = ostg.next()
            if sem not in out_sems:
                out_sems.append(sem)
            for hb in range(2):
                b = nbank()
                for j in range(4):
                    c = hb * 4 + j
                    S.op(PE, lambda e, b=b, j=j, c=c, i=i: e.transpose(
                        out=PS[b][:, j * 128:(j + 1) * 128], in_=xT[:, c, i * 128:(i + 1) * 128], identity=identf),
                        reads=[R_x[i // 4], R_const], writes=[R_PS[b]])
                copy_op(evac_eng(), buf[:, hb * 512:(hb + 1) * 512], PS[b][:, :], [R_PS[b]], [R_b])
            S.dma(SP if i % 2 == 0 else POOL,
                  [lambda e, buf=buf, i=i: e.dma_start(out=out_d[i * 128:(i + 1) * 128, :], in_=buf)], sem,
                  reads=[R_b])
        for sem in out_sems:
            SP.prog.append(lambda e, sem=sem, v=S.dma_val[sem]: e.wait_ge(sem, v))
        S.emit()
    return nc


def _host_constants():
    bf = ml_dtypes.bfloat16
    identf = np.eye(128, dtype=np.float32)
    kk = np.arange(128)[:, None]
    qq = np.arange(128)[None, :]
    negmask = np.where(qq >= kk, 0.0, -30000.0).astype(np.float32)
    c_bf = np.concatenate([identf, np.ones((128, 128), np.float32), negmask,
                           np.zeros((128, 128), np.float32)], axis=1).astype(bf)
    pos = np.arange(T)
    hi = (pos // 128) * 128
    lo = pos % 128
    qaug = np.stack([hi, lo, np.ones(T), np.ones(T)]).astype(np.float32).astype(bf)
    kaug = []
    for h in range(8):
        s = 2.0 ** (-(h + 1))
        kaug.append(np.stack([-s * np.ones(T), -s * np.ones(T), s * hi, s * lo]))
    kaug = np.concatenate(kaug, 0).astype(np.float32).astype(bf)
    return identf, c_bf, qaug, kaug


def _pack_inputs(inp):
    def col(v):
        return np.ascontiguousarray(np.asarray(v, np.float32).reshape(-1, 128).T)
    cols = [col(inp["ffn1_pre_g"][0]), col(inp["ffn1_post_g"][0]), col(inp["mix_pre_g"][0]),
            col(inp["mix_post_g"][0]), col(inp["ffn2_pre_g"][0]), col(inp["ffn2_post_g"][0]),
            col(inp["mem_g"][0])]
    for t in range(4):
        cols.append(col(inp["lru_conv_w"][0, t]))
    cols += [col(inp["lru_conv_b"][0]), col(inp["lru_b_a"][0]), col(inp["lru_b_x"][0]), col(inp["lru_lambda"][0])]
    cols.append(col(inp["b_branch_gate"][0]))
    pvec = np.ascontiguousarray(np.concatenate(cols, axis=1))
    assert pvec.shape == (128, NV_IN), pvec.shape
    lamv = np.concatenate([inp["da_lambda_q1"][0], inp["da_lambda_k1"][0], inp["da_lambda_q2"][0],
                           inp["da_lambda_k2"][0]]).astype(np.float32)[None, :]
    identf, c_bf, qaug, kaug = _host_constants()
    f32 = lambda a: np.ascontiguousarray(np.asarray(a, np.float32))
    shared = {
        "f1_wg": f32(inp["ffn1_w_gate"][0]), "f1_wu": f32(inp["ffn1_w_up"][0]), "f1_wd": f32(inp["ffn1_w_down"][0]),
        "f2_wg": f32(inp["ffn2_w_gate"][0]), "f2_wu": f32(inp["ffn2_w_up"][0]), "f2_wd": f32(inp["ffn2_w_down"][0]),
        "w_in": f32(inp["w_in"][0]), "w_da_out": f32(inp["w_da_out"][0]), "w_lru_out": f32(inp["w_lru_out"][0]),
        "w_ca_out": f32(inp["w_ca_out"][0]), "w_bg": f32(inp["w_branch_gate"][0]), "w_mix": f32(inp["w_mix_out"][0]),
        "w_mem_kv": f32(inp["w_mem_kv"][0]),
        "lru_wa": f32(np.asarray(inp["lru_w_a"][0]).reshape(D, 128)),
        "lru_wx": f32(np.asarray(inp["lru_w_x"][0]).reshape(D, 128)),
        "pvec": pvec, "lamv": lamv, "headg": f32(inp["da_head_g"]),
        "c_identf": identf, "c_bf": c_bf, "qaug": qaug, "kaug": kaug,
    }
    return shared


_NC_CACHE = {}


def kernel(**inputs):
    inputs = {k: np.asarray(v) for k, v in inputs.items()}
    dbg = os.environ.get("MK_STAGES", "all")
    if dbg not in _NC_CACHE:
        _NC_CACHE[dbg] = build_program(dbg)
    nc = _NC_CACHE[dbg]
    shared = _pack_inputs(inputs)
    x = np.ascontiguousarray(inputs["x"], dtype=np.float32)
    mem = np.ascontiguousarray(inputs["mem"], dtype=np.float32)
    in_maps = []
    for b in range(NCORES):
        m = dict(shared)
        m["x"] = x[b]
        m["mem"] = mem[b]
        in_maps.append(m)
    res = run_bass_kernel_spmd(nc, in_maps, core_ids=list(range(NCORES)))
    out = np.stack([np.asarray(r["out"], dtype=np.float32) for r in res.results], axis=0)
    return out
```

```python
import contextlib
import os

import numpy as np
import ml_dtypes
import concourse.bass as bass
import concourse.mybir as mybir
from concourse.bass_utils import run_bass_kernel_spmd

F32 = mybir.dt.float32
BF16 = mybir.dt.bfloat16
AF = mybir.ActivationFunctionType
ALU = mybir.AluOpType

T = 2048
D = 1024
DC = 8
FF = 2816
FC = 22
NMEM = 256
EPS = 1e-6
NCORES = 8

PV = dict(g_f1pre=0, g_f1post=8, g_mixpre=16, g_mixpost=24, g_f2pre=32, g_f2post=40, g_mem=48,
          convw=56, convb=88, ba=96, bx=104, lam=112, bbg=120)
NV_IN = 144
PV_DER = dict(pg_f1=144, pg_f2=152, coef=160, tmp=168, hcoef=176, hba=184, hbx=192)
NV = 200


class Res:
    __slots__ = ("name", "w", "r")

    def __init__(self, name):
        self.name = name
        self.w = {}
        self.r = {}


class Eng:
    def __init__(self, name, sem, is_pe=False):
        self.name = name
        self.sem = sem
        self.cnt = 0
        self.waited = {}
        self.prog = []
        self.is_pe = is_pe


class Sched:
    def __init__(self, nc, stack):
        self.nc = nc
        self.stack = stack
        self.pe = Eng("pe", self._sem("s_pe"), is_pe=True)
        self.act = Eng("act", self._sem("s_act"))
        self.dve = Eng("dve", self._sem("s_dve"))
        self.pool = Eng("pool", self._sem("s_pool"))
        self.sp = Eng("sp", self._sem("s_sp"))
        self.engs = [self.pe, self.act, self.dve, self.pool, self.sp]
        self.dma_val = {}
        self.deferred = []

    def defer(self, thunk):
        self.deferred.append(thunk)

    def drain(self, k=None):
        n = len(self.deferred) if k is None else min(k, len(self.deferred))
        for _ in range(n):
            self.deferred.pop(0)()

    def _sem(self, name):
        return self.stack.enter_context(self.nc.semaphore(name))

    def new_dma_sem(self, name):
        s = self._sem(name)
        self.dma_val[s] = 0
        return s

    def _deps(self, eng, reads, writes):
        deps = {}

        def need(sem, val, raw):
            if sem is eng.sem and eng.is_pe:
                return
            if deps.get(sem, 0) < val:
                deps[sem] = val
        for r in reads:
            for s, v in r.w.items():
                need(s, v, True)
        for w in writes:
            for s, v in w.w.items():
                need(s, v, False)
            for s, v in w.r.items():
                need(s, v, False)
        pend = []
        for s, v in deps.items():
            if eng.waited.get(s, 0) < v:
                eng.waited[s] = v
                pend.append((s, v))
        for s, v in pend[:-1]:
            eng.prog.append(lambda e, s=s, v=v: e.wait_ge(s, v))
        return pend[-1] if pend else None

    def op(self, eng, fn, reads=(), writes=()):
        fw = self._deps(eng, reads, writes)
        eng.cnt += 1
        v = eng.cnt
        sem = eng.sem
        if fw is None:
            eng.prog.append(lambda e, fn=fn, sem=sem: fn(e).then_inc(sem, 1))
        else:
            eng.prog.append(lambda e, fn=fn, sem=sem, fw=fw: fn(e)._wait_ge(fw[0], fw[1]).then_inc(sem, 1))
        for r in reads:
            if r.r.get(sem, 0) < v:
                r.r[sem] = v
        for w in writes:
            w.w = {sem: v}
            w.r = {}

    def dma(self, eng, fns, sem, reads=(), writes=()):
        fw = self._deps(eng, reads, writes)
        if fw is not None:
            eng.prog.append(lambda e, fw=fw: e.wait_ge(fw[0], fw[1]))
        for fn in fns:
            self.dma_val[sem] += 16
            eng.prog.append(lambda e, fn=fn, sem=sem: fn(e).then_inc(sem, 16))
        v = self.dma_val[sem]
        for r in reads:
            if r.r.get(sem, 0) < v:
                r.r[sem] = v
        for w in writes:
            w.w = {sem: v}
            w.r = {}

    def barrier(self):
        for e in self.engs:
            for o in (self.pe, self.act, self.dve):
                if o is e and e.is_pe:
                    continue
                if o.cnt > 0 and e.waited.get(o.sem, 0) < o.cnt:
                    e.waited[o.sem] = o.cnt
                    e.prog.append(lambda q, s=o.sem, v=o.cnt: q.wait_ge(s, v))

    def emit(self):
        with self.nc.Block() as block:
            @block.tensor
            def _(e):
                for f in self.pe.prog:
                    f(e)

            @block.scalar
            def _(e):
                for f in self.act.prog:
                    f(e)

            @block.vector
            def _(e):
                for f in self.dve.prog:
                    f(e)

            @block.gpsimd
            def _(e):
                for f in self.pool.prog:
                    f(e)

            @block.sync
            def _(e):
                for f in self.sp.prog:
                    f(e)


class Arena:
    def __init__(self, t, size):
        self.t = t
        self.size = size
        self.off = 0

    def f32(self, n):
        a = self.t[:, self.off:self.off + n]
        self.off += n
        assert self.off <= self.size, ("arena overflow", self.off, self.size)
        return a

    def bf16(self, n):
        w = (n + 1) // 2
        a = self.t[:, self.off:self.off + w].bitcast(BF16)
        self.off += w
        assert self.off <= self.size, ("arena overflow", self.off, self.size)
        return a


class Slots:
    def __init__(self, items):
        self.items = items
        self.i = 0

    def next(self):
        it = self.items[self.i % len(self.items)]
        self.i += 1
        return it


def build_program(dbg=None):
    nc = bass.Bass("TRN2", target_bir_lowering=False)

    def din(name, shape, dt=F32):
        return nc.dram_tensor(name, list(shape), dt, kind="ExternalInput").ap()

    x_d = din("x", [T, D])
    mem_d = din("mem", [NMEM, D])
    out_d = nc.dram_tensor("out", [T, D], F32, kind="ExternalOutput").ap()
    Wd_ = {}
    for nm, shp in [("f1_wg", [D, FF]), ("f1_wu", [D, FF]), ("f1_wd", [FF, D]),
                    ("f2_wg", [D, FF]), ("f2_wu", [D, FF]), ("f2_wd", [FF, D]),
                    ("w_in", [D, 6144]), ("w_da_out", [D, D]), ("w_lru_out", [D, D]), ("w_ca_out", [D, D]),
                    ("w_bg", [D, 3 * D]), ("w_mix", [D, D]), ("w_mem_kv", [D, 2 * D]),
                    ("lru_wa", [D, 128]), ("lru_wx", [D, 128])]:
        Wd_[nm] = din(nm, shp)
    pvec_d = din("pvec", [128, NV_IN])
    lamv_d = din("lamv", [1, 256])
    headg_d = din("headg", [1, 128])
    c_identf_d = din("c_identf", [128, 128])
    c_bf_d = din("c_bf", [128, 4 * 128], BF16)
    qaug_d = din("qaug", [4, T], BF16)
    kaug_d = din("kaug", [32, T], BF16)

    ARENA_WORDS = 53200
    with contextlib.ExitStack() as st:
        S = Sched(nc, st)
        arena_t = st.enter_context(nc.sbuf_tensor("arena", [128, ARENA_WORDS], F32))
        A = Arena(arena_t, ARENA_WORDS)
        PSALL = st.enter_context(nc.psum_tensor("psall", [128, 8 * 512], F32))
        PS = [PSALL[:, i * 512:(i + 1) * 512] for i in range(8)]
        R_PS = [Res(f"ps{i}") for i in range(8)]

        PE, ACT, DVE, POOL, SP = S.pe, S.act, S.dve, S.pool, S.sp

        xT = A.f32(DC * T).rearrange("p (c t) -> p c t", c=DC)
        R_x = [Res(f"x{g}") for g in range(4)]
        identf = A.f32(128)
        cbf = A.bf16(4 * 128)
        identb, onesb, negmask, zerosb = cbf[:, 0:128], cbf[:, 128:256], cbf[:, 256:384], cbf[:, 384:512]
        pv = A.f32(NV)
        hg = A.f32(128)
        lamv = arena_t[:, ARENA_WORDS - 256:ARENA_WORDS]
        misc = A.f32(16)
        R_const = Res("const")
        R_pv = Res("pv")
        R_misc = Res("misc")
        epsc = misc[:, 0:1]
        neglam = misc[:, 1:2]
        qtr = misc[:, 4:5]
        PERSIST = A.off

        sem_c = S.new_dma_sem("d_const")
        S.dma(SP, [lambda e: e.dma_start(out=identf, in_=c_identf_d),
                   lambda e: e.dma_start(out=cbf, in_=c_bf_d)], sem_c, writes=[R_const])
        sem_pv = S.new_dma_sem("d_pv")
        S.dma(SP, [lambda e: e.dma_start(out=pv[:, 0:NV_IN], in_=pvec_d),
                   lambda e: e.dma_start(out=hg, in_=headg_d.partition_broadcast(128)),
                   lambda e: e.dma_start(out=lamv, in_=lamv_d.partition_broadcast(128))], sem_pv, writes=[R_pv])

        def pcol(name, c=0, n=1):
            o = (PV[name] if name in PV else PV_DER[name]) + c
            return pv[:, o:o + n]

        S.op(DVE, lambda e: e.memset(epsc, EPS), writes=[R_misc])
        S.op(DVE, lambda e: e.memset(qtr, 0.25), writes=[R_misc])
        S.op(DVE, lambda e: e.tensor_scalar(out=pcol("pg_f1", 0, 8), in0=pcol("g_f1post", 0, 8), scalar1=0.5,
                                            scalar2=None, op0=ALU.mult), reads=[R_pv], writes=[R_pv])
        S.op(DVE, lambda e: e.tensor_scalar(out=pcol("pg_f2", 0, 8), in0=pcol("g_f2post", 0, 8), scalar1=0.5,
                                            scalar2=None, op0=ALU.mult), reads=[R_pv], writes=[R_pv])
        S.op(DVE, lambda e: e.tensor_scalar(out=hg, in0=hg, scalar1=0.8, scalar2=None, op0=ALU.mult),
             reads=[R_pv], writes=[R_pv])
        S.op(ACT, lambda e: e.activation(out=pcol("tmp", 0, 8), in_=pcol("lam", 0, 8), func=AF.Exp, scale=-1.0),
             reads=[R_pv], writes=[R_pv])
        S.op(ACT, lambda e: e.activation(out=pcol("tmp", 0, 8), in_=pcol("tmp", 0, 8), func=AF.Ln, bias=1.0),
             reads=[R_pv], writes=[R_pv])
        S.op(DVE, lambda e: e.tensor_scalar(out=pcol("coef", 0, 8), in0=pcol("tmp", 0, 8), scalar1=-8.0,
                                            scalar2=None, op0=ALU.mult), reads=[R_pv], writes=[R_pv])
        for dst, src in (("hcoef", "coef"), ("hba", "ba"), ("hbx", "bx")):
            S.op(DVE, lambda e, dst=dst, src=src: e.tensor_scalar(out=pcol(dst, 0, 8), in0=pcol(src, 0, 8), scalar1=0.5,
                                                                 scalar2=None, op0=ALU.mult), reads=[R_pv], writes=[R_pv])
        S.op(DVE, lambda e: e.scalar_tensor_tensor(out=lamv[:, 0:64], in0=lamv[:, 0:64], scalar=1.0,
                                                   in1=lamv[:, 64:128], op0=ALU.mult, op1=ALU.mult,
                                                   accum_out=misc[:, 2:3]), reads=[R_pv], writes=[R_pv, R_misc])
        S.op(DVE, lambda e: e.scalar_tensor_tensor(out=lamv[:, 128:192], in0=lamv[:, 128:192], scalar=1.0,
                                                   in1=lamv[:, 192:256], op0=ALU.mult, op1=ALU.mult,
                                                   accum_out=misc[:, 3:4]), reads=[R_pv], writes=[R_pv, R_misc])
        S.op(ACT, lambda e: e.activation(out=misc[:, 2:4], in_=misc[:, 2:4], func=AF.Exp),
             reads=[R_misc], writes=[R_misc])
        S.op(DVE, lambda e: e.tensor_tensor(out=neglam, in0=misc[:, 3:4], in1=misc[:, 2:3], op=ALU.subtract),
             reads=[R_misc], writes=[R_misc])
        S.op(DVE, lambda e: e.tensor_scalar(out=neglam, in0=neglam, scalar1=-0.2, scalar2=None, op0=ALU.add),
             reads=[R_misc], writes=[R_misc])

        ring = [0]

        def nbank(lo=0, hi=6):
            b = lo + ring[0] % (hi - lo)
            ring[0] += 1
            return b

        flip = [0]

        def evac_eng():
            flip[0] += 1
            return ACT if flip[0] % 2 else DVE

        def copy_op(eng, out, in_, reads, writes, scale=None):
            if eng is ACT:
                if scale is None:
                    S.op(ACT, lambda e: e.activation(out=out, in_=in_, func=AF.Copy), reads=reads, writes=writes)
                else:
                    S.op(ACT, lambda e: e.activation(out=out, in_=in_, func=AF.Copy, scale=scale),
                         reads=reads, writes=writes)
            else:
                if scale is None:
                    S.op(DVE, lambda e: e.tensor_copy(out=out, in_=in_), reads=reads, writes=writes)
                else:
                    S.op(DVE, lambda e: e.tensor_scalar(out=out, in0=in_, scalar1=scale, scalar2=None,
                                                        op0=ALU.mult), reads=reads, writes=writes)

        def wload(dst, src, sem, R, max_last=None):
            if max_last is None:
                S.dma(POOL, [lambda e: e.dma_start(out=dst, in_=src)], sem, writes=[R])
            else:
                S.dma(POOL, [lambda e: e.dma_start(out=dst, in_=src, max_dma_last_dim=max_last)], sem, writes=[R])

        def rms_rstd(src_chunks, src_res, n, sq_slots, rstd_ap, R_rstd, bank, ndim=D):
            nchunk = len(src_chunks)
            for c, ap in enumerate(src_chunks):
                sq, R_sq = sq_slots.next()
                S.op(ACT, lambda e, sq=sq, ap=ap: e.activation(out=sq[:, 0:n], in_=ap, func=AF.Square),
                     reads=src_res, writes=[R_sq])
                S.op(PE, lambda e, sq=sq, c=c: e.matmul(PS[bank][:, 0:n], lhsT=onesb, rhs=sq[:, 0:n],
                                                        start=(c == 0), stop=(c == nchunk - 1)),
                     reads=[R_sq, R_const], writes=[R_PS[bank]])
            S.op(ACT, lambda e: e.activation(out=rstd_ap, in_=PS[bank][:, 0:n], func=AF.Sqrt,
                                             scale=1.0 / ndim, bias=epsc),
                 reads=[R_PS[bank], R_misc], writes=[R_rstd])
            S.op(DVE, lambda e: e.reciprocal(out=rstd_ap, in_=rstd_ap), reads=[R_rstd], writes=[R_rstd])

        A.off = PERSIST
        stg = []
        for i in range(8):
            stg.append((A.f32(D), Res(f"stg{i}"), S.new_dma_sem(f"d_stg{i}")))
        stg = Slots(stg)
        for i in range(16):
            buf, R_b, sem = stg.next()
            S.dma(SP if i % 2 == 0 else POOL,
                  [lambda e, buf=buf, i=i: e.dma_start(out=buf, in_=x_d[i * 128:(i + 1) * 128, :])],
                  sem, writes=[R_b])
            for hb in range(2):
                b = nbank()
                for j in range(4):
                    c = hb * 4 + j
                    S.op(PE, lambda e, b=b, j=j, c=c, buf=buf: e.transpose(
                        out=PS[b][:, j * 128:(j + 1) * 128], in_=buf[:, c * 128:(c + 1) * 128], identity=identf),
                        reads=[R_b, R_const], writes=[R_PS[b]])
                copy_op(evac_eng(), xT[:, hb * 4:(hb + 1) * 4, i * 128:(i + 1) * 128],
                        PS[b][:, :].rearrange("p (c t) -> p c t", c=4), [R_PS[b]], [R_x[i // 4]])
        S.barrier()

        def ffn_stage(pre, pgname, wg_d, wu_d, wd_d, tag, last=False):
            A.off = PERSIST
            hTh = A.bf16(DC * 1024).rearrange("p (c t) -> p c t", c=DC)
            R_h = [Res(f"{tag}h{g}") for g in range(2)]
            actT = A.bf16(FC * 1024).rearrange("p (c t) -> p c t", c=FC)
            R_a = [Res(f"{tag}a{g}") for g in range(2)]
            fT = A.f32(DC * 1024).rearrange("p (c t) -> p c t", c=DC)
            R_f = [Res(f"{tag}f{g}") for g in range(2)]
            wg_s = Slots([(A.bf16(DC * 256).rearrange("p (c n) -> p c n", c=DC), Res(f"{tag}wg{i}"),
                           S.new_dma_sem(f"d_{tag}wg{i}")) for i in range(2)])
            wu_s = Slots([(A.bf16(DC * 256).rearrange("p (c n) -> p c n", c=DC), Res(f"{tag}wu{i}"),
                           S.new_dma_sem(f"d_{tag}wu{i}")) for i in range(2)])
            wd_s = Slots([(A.bf16(FC * 128).rearrange("p (c n) -> p c n", c=FC), Res(f"{tag}wd{i}"),
                           S.new_dma_sem(f"d_{tag}wd{i}")) for i in range(2)])
            sq_s = Slots([(A.bf16(512), Res(f"{tag}sq{i}")) for i in range(4)])
            sg_s = Slots([(A.f32(512), Res(f"{tag}sg{i}")) for i in range(2)])
            rstd = A.f32(1024)
            R_rstd = [Res(f"{tag}rstd{g}") for g in range(2)]
            wg_v = wg_d.rearrange("(c p) n -> p c n", p=128)
            wu_v = wu_d.rearrange("(c p) n -> p c n", p=128)
            wd_v = wd_d.rearrange("(c p) n -> p c n", p=128)
            rstd_post = A.f32(1024)
            R_rpost = [Res(f"{tag}rpost{g}") for g in range(2)]

            def prenorm_rstd(half):
                t0 = half * 1024
                for g in range(2):
                    gg = half * 2 + g
                    tok = slice(t0 + g * 512, t0 + (g + 1) * 512)
                    rs = rstd[:, g * 512:(g + 1) * 512]
                    rms_rstd([xT[:, c, tok] for c in range(DC)], [R_x[gg]], 512, sq_s, rs, R_rstd[g], 6)

            def prenorm_apply(half):
                t0 = half * 1024
                for g in range(2):
                    gg = half * 2 + g
                    tok = slice(t0 + g * 512, t0 + (g + 1) * 512)
                    loc = slice(g * 512, (g + 1) * 512)
                    rs = rstd[:, loc]
                    for c in range(DC):
                        S.op(DVE, lambda e, c=c, tok=tok, loc=loc, rs=rs: e.scalar_tensor_tensor(
                            out=hTh[:, c, loc], in0=xT[:, c, tok], scalar=pcol(pre, c), in1=rs,
                            op0=ALU.mult, op1=ALU.mult), reads=[R_x[gg], R_pv, R_rstd[g]], writes=[R_h[g]])

            def gateup(half, mid_hook=None, drain=0):
                for gi in range(11):
                    wg, R_wg, s_wg = wg_s.next()
                    wu, R_wu, s_wu = wu_s.next()
                    wload(wg, wg_v[:, :, gi * 256:(gi + 1) * 256], s_wg, R_wg)
                    wload(wu, wu_v[:, :, gi * 256:(gi + 1) * 256], s_wu, R_wu)
                    if gi == 6 and mid_hook is not None:
                        mid_hook()
                    for cc in range(2):
                        c = gi * 2 + cc
                        for g in range(2):
                            loc = slice(g * 512, (g + 1) * 512)
                            bg, bu = nbank(), nbank()
                            for k in range(DC):
                                S.op(PE, lambda e, bg=bg, k=k, cc=cc, loc=loc, wg=wg: e.matmul(
                                    PS[bg][:, :], lhsT=wg[:, k, cc * 128:(cc + 1) * 128], rhs=hTh[:, k, loc],
                                    start=(k == 0), stop=(k == DC - 1)), reads=[R_wg, R_h[g]], writes=[R_PS[bg]])
                            for k in range(DC):
                                S.op(PE, lambda e, bu=bu, k=k, cc=cc, loc=loc, wu=wu: e.matmul(
                                    PS[bu][:, :], lhsT=wu[:, k, cc * 128:(cc + 1) * 128], rhs=hTh[:, k, loc],
                                    start=(k == 0), stop=(k == DC - 1)), reads=[R_wu, R_h[g]], writes=[R_PS[bu]])
                            sg, R_sg = sg_s.next()
                            S.op(ACT, lambda e, sg=sg, bg=bg: e.activation(out=sg, in_=PS[bg][:, :], func=AF.Silu),
                                 reads=[R_PS[bg]], writes=[R_sg])
                            S.op(DVE, lambda e, sg=sg, bu=bu, c=c, loc=loc: e.tensor_tensor(
                                out=actT[:, c, loc], in0=sg, in1=PS[bu][:, :], op=ALU.mult),
                                reads=[R_sg, R_PS[bu]], writes=[R_a[g]])
                            if drain:
                                S.drain(drain)

            def down(half, gouter=False):
                lag = []
                if gouter:
                    order = [(dc, g) for g in range(2) for dc in range(DC)]
                else:
                    order = [(dc, g) for dc in range(DC) for g in range(2)]
                wd = R_wd = None
                for (dc, g) in order:
                    if gouter or g == 0:
                        wd, R_wd, s_wd = wd_s.next()
                        wload(wd, wd_v[:, :, dc * 128:(dc + 1) * 128], s_wd, R_wd)
                    loc = slice(g * 512, (g + 1) * 512)
                    b = nbank()
                    while len(lag) > 1:
                        lag.pop(0)()
                    for c in range(FC):
                        S.op(PE, lambda e, b=b, c=c, loc=loc, wd=wd: e.matmul(
                            PS[b][:, :], lhsT=wd[:, c, :], rhs=actT[:, c, loc],
                            start=(c == 0), stop=(c == FC - 1)), reads=[R_wd, R_a[g]], writes=[R_PS[b]])
                    S.op(DVE, lambda e, b=b, dc=dc, loc=loc: e.tensor_copy(out=fT[:, dc, loc], in_=PS[b][:, :]),
                         reads=[R_PS[b]], writes=[R_f[g]])
                    sq, R_sq = sq_s.next()
                    S.op(ACT, lambda e, sq=sq, dc=dc, loc=loc: e.activation(out=sq, in_=fT[:, dc, loc],
                                                                          func=AF.Square),
                         reads=[R_f[g]], writes=[R_sq])
                    lag.append(lambda sq=sq, R_sq=R_sq, g=g, dc=dc: S.op(PE, lambda e: e.matmul(
                        PS[6 + g][:, :], lhsT=onesb, rhs=sq, start=(dc == 0), stop=(dc == DC - 1)),
                        reads=[R_sq, R_const], writes=[R_PS[6 + g]]))
                    if gouter:
                        S.drain(3)
                        if g == 0 and dc == DC - 1:
                            while lag:
                                lag.pop(0)()
                            for th in post_thunks(half, 0):
                                S.defer(th)
                while lag:
                    lag.pop(0)()

            def post_thunks(half, only_g=None):
                t0 = half * 1024
                th = []
                for g in ((0, 1) if only_g is None else (only_g,)):
                    gg = half * 2 + g
                    tok = slice(t0 + g * 512, t0 + (g + 1) * 512)
                    loc = slice(g * 512, (g + 1) * 512)
                    rs = rstd_post[:, loc]
                    th.append(lambda rs=rs, g=g: S.op(ACT, lambda e: e.activation(
                        out=rs, in_=PS[6 + g][:, :], func=AF.Sqrt, scale=1.0 / D, bias=epsc),
                        reads=[R_PS[6 + g], R_misc], writes=[R_rpost[g]]))
                    th.append(lambda rs=rs, g=g: S.op(DVE, lambda e: e.reciprocal(out=rs, in_=rs),
                                                      reads=[R_rpost[g]], writes=[R_rpost[g]]))
                    for dc in range(DC):
                        th.append(lambda dc=dc, loc=loc, rs=rs, g=g: S.op(DVE, lambda e: e.tensor_tensor(
                            out=fT[:, dc, loc], in0=fT[:, dc, loc], in1=rs, op=ALU.mult),
                            reads=[R_f[g], R_rpost[g]], writes=[R_f[g]]))
                        th.append(lambda dc=dc, loc=loc, tok=tok, g=g, gg=gg: S.op(DVE, lambda e: e.scalar_tensor_tensor(
                            out=xT[:, dc, tok], in0=fT[:, dc, loc], scalar=pcol(pgname, dc), in1=xT[:, dc, tok],
                            op0=ALU.mult, op1=ALU.add), reads=[R_f[g], R_pv, R_x[gg]], writes=[R_x[gg]]))
                return th

            prenorm_rstd(0)
            prenorm_apply(0)
            gateup(0, mid_hook=lambda: prenorm_rstd(1))
            prenorm_apply(1)
            down(0)
            for th in post_thunks(0):
                S.defer(th)
            gateup(1, drain=1)
            S.drain()
            down(1)
            for th in post_thunks(1):
                th()
            if not last:
                S.barrier()

        def mixer_stage(upto=None):
            A.off = PERSIST
            hT = A.bf16(DC * T).rearrange("p (c t) -> p c t", c=DC)
            R_h = [Res(f"mh{g}") for g in range(4)]
            O1BASE = A.off
            O1 = A.bf16(DC * T).rearrange("p (c t) -> p c t", c=DC)
            R_o1 = [Res(f"o1_{g}") for g in range(4)]
            YBASE = A.off
            O2 = A.bf16(DC * T).rearrange("p (c t) -> p c t", c=DC)
            R_o2 = [Res(f"o2_{g}") for g in range(4)]
            ZBASE = A.off
            w_in_v = Wd_["w_in"].rearrange("(c p) n -> p c n", p=128)
            w_bg_v = Wd_["w_bg"].rearrange("(c p) n -> p c n", p=128)

            sq_s = Slots([(A.bf16(512), Res(f"m0sq{i}")) for i in range(2)])
            rstd_s = Slots([(A.f32(512), Res(f"m0rs{i}")) for i in range(2)])
            for g in range(4):
                tok = slice(g * 512, (g + 1) * 512)
                rs, R_rs = rstd_s.next()
                rms_rstd([xT[:, c, tok] for c in range(DC)], [R_x[g]], 512, sq_s, rs, R_rs, 6 + g % 2)
                for c in range(DC):
                    S.op(DVE, lambda e, c=c, tok=tok, rs=rs: e.scalar_tensor_tensor(
                        out=hT[:, c, tok], in0=xT[:, c, tok], scalar=pcol("g_mixpre", c), in1=rs,
                        op0=ALU.mult, op1=ALU.mult), reads=[R_x[g], R_pv, R_rs], writes=[R_h[g]])
            S.barrier()

            A.off = YBASE
            qk = []
            for sset in range(2):
                d_ = {}
                for nm in ("QA", "QB", "KA", "KB"):
                    d_[nm] = A.bf16(T)
                    d_["R_" + nm] = Res(f"{nm}{sset}")
                d_["V1"] = A.bf16(16 * 130).rearrange("p (t n) -> p t n", t=16)
                d_["R_V"] = Res(f"V1_{sset}")
                d_["s_aug"] = S.new_dma_sem(f"d_kaug{sset}")
                qk.append(d_)
            e_s = Slots([(A.bf16(1024), Res(f"E{i}")) for i in range(3)])
            spair = [0]
            wq_s = Slots([(A.bf16(DC * 128).rearrange("p (c n) -> p c n", c=DC), Res(f"wqkv{i}"),
                           S.new_dma_sem(f"d_wqkv{i}")) for i in range(6)])
            o_all = A.f32(16 * 128).rearrange("p (t n) -> p t n", t=16)
            R_oall = Res("o_all")
            on_all = A.bf16(16 * 128).rearrange("p (t n) -> p t n", t=16)
            R_on = Res("on_all")
            oraw_s = Slots([(A.f32(4 * 130).rearrange("p (t m n) -> p t m n", t=2, m=2), Res(f"oraw{i}")) for i in range(2)])
            pending_T = []
            t1_s = Slots([(A.f32(128), Res(f"t1_{i}")) for i in range(2)])
            sm_s = Slots([(A.f32(4), Res(f"sm{i}")) for i in range(4)])
            ssq = A.f32(16)
            R_ssq = Res("ssq")
            junk = A.f32(128)
            R_junk = Res("junk")
            for qi_, d_ in enumerate(qk):
                sem_qaug = S.new_dma_sem(f"d_qaug{qi_}")
                for nm in ("QA", "QB", "KA", "KB"):
                    S.op(DVE, lambda e, ap=d_[nm]: e.memset(ap, 0.0), writes=[d_["R_" + nm]])
                S.op(DVE, lambda e, ap=d_["V1"]: e.memset(ap, 1.0), writes=[d_["R_V"]])
                S.dma(SP, [lambda e, d_=d_: e.dma_start(out=d_["QA"][64:68, :], in_=qaug_d),
                           lambda e, d_=d_: e.dma_start(out=d_["QB"][0:4, :], in_=qaug_d)], sem_qaug,
                      writes=[d_["R_QA"], d_["R_QB"]])
            def emit_proj(h):
                d_ = qk[h % 2]
                QA, QB, KA, KB, V1 = d_["QA"], d_["QB"], d_["KA"], d_["KB"], d_["V1"]
                R_QA, R_QB, R_KA, R_KB, R_V = d_["R_QA"], d_["R_QB"], d_["R_KA"], d_["R_KB"], d_["R_V"]
                S.dma(SP, [lambda e, h=h, KA=KA: e.dma_start(out=KA[64:68, :], in_=kaug_d[h * 4:(h + 1) * 4, :]),
                           lambda e, h=h, KB=KB: e.dma_start(out=KB[0:4, :], in_=kaug_d[h * 4:(h + 1) * 4, :])],
                      d_["s_aug"], writes=[R_KA, R_KB])
                wq, R_wq, s_wq = wq_s.next()
                wk, R_wk, s_wk = wq_s.next()
                wv, R_wv, s_wv = wq_s.next()
                wload(wq, w_in_v[:, :, h * 128:(h + 1) * 128], s_wq, R_wq)
                wload(wk, w_in_v[:, :, 1024 + h * 128:1024 + (h + 1) * 128], s_wk, R_wk)
                wload(wv, w_in_v[:, :, 2048 + h * 128:2048 + (h + 1) * 128], s_wv, R_wv)
                for (w_, R_w, XA, XB, R_XA, R_XB, sc) in ((wq, R_wq, QA, QB, R_QA, R_QB, 0.125),
                                                          (wk, R_wk, KA, KB, R_KA, R_KB, None)):
                    for g in range(4):
                        tok = slice(g * 512, (g + 1) * 512)
                        b = nbank(0, 6)
                        for k in range(DC):
                            S.op(PE, lambda e, b=b, k=k, tok=tok, w_=w_: e.matmul(
                                PS[b][:, :], lhsT=w_[:, k, :], rhs=hT[:, k, tok], start=(k == 0), stop=(k == DC - 1)),
                                reads=[R_w, R_h[g]], writes=[R_PS[b]])
                        copy_op(DVE if g % 2 else ACT, XA[0:64, tok], PS[b][0:64, :], [R_PS[b]], [R_XA], scale=sc)
                        copy_op(DVE, XB[64:128, tok], PS[b][64:128, :], [R_PS[b]], [R_XB], scale=sc)
                psv = PSALL[:, :].rearrange("p (b n) -> p b n", n=512)
                for tq in range(4):
                    for k in range(DC):
                        for t in range(4):
                            tl = tq * 4 + t
                            S.op(PE, lambda e, k=k, t=t, tl=tl, wv=wv: e.matmul(
                                PS[t][:, 0:128], lhsT=hT[:, k, tl * 128:(tl + 1) * 128], rhs=wv[:, k, :],
                                start=(k == 0), stop=(k == DC - 1)), reads=[R_wv, R_h[tq]], writes=[R_PS[t]])
                    copy_op(DVE, V1[:, tq * 4:(tq + 1) * 4, 0:128], psv[:, 0:4, 0:128],
                            [R_PS[0], R_PS[1], R_PS[2], R_PS[3]], [R_V])

            emit_proj(0)
            for h in range(8):
                d_ = qk[h % 2]
                QA, QB, KA, KB, V1 = d_["QA"], d_["QB"], d_["KA"], d_["KB"], d_["V1"]
                R_QA, R_QB, R_KA, R_KB, R_V = d_["R_QA"], d_["R_QB"], d_["R_KA"], d_["R_KB"], d_["R_V"]
                units = []
                for p in range(8):
                    for m in range(2):
                        for g0 in range(0, p + 1, 2):
                            units.append((p, m, tuple(g for g in (g0, g0 + 1) if g <= p)))
                ust = {}
                oraw_of = {}

                def emit_S(u, KA=KA, KB=KB, QA=QA, QB=QB, R_KA=R_KA, R_KB=R_KB, R_QA=R_QA, R_QB=R_QB):
                    p, m, gs = u
                    i0, i1 = 2 * p, 2 * p + 1
                    Kx, Qx, R_Kx, R_Qx = (KA, QA, R_KA, R_QA) if m == 0 else (KB, QB, R_KB, R_QB)
                    base = 2 * (spair[0] % 3)
                    spair[0] += 1
                    allblocks = []
                    ntot = 0
                    per_g = []
                    for gi, g in enumerate(gs):
                        if g < p:
                            per_g.append([(gi, g, 2 * g, 0, 256), (gi, g, 2 * g + 1, 256, 256)])
                        else:
                            per_g.append([(gi, g, i0, 0, 256), (gi, g, i1, 256, 128)])
                        ntot = gi * 512 + (512 if g < p else 384)
                    order = []
                    for bi_ in range(2):
                        for lst in per_g:
                            order.append(lst[bi_])
                    for (gi, g, j, coff, ncol) in order:
                        sb_ = base + gi
                        qsl = slice(i0 * 128, i0 * 128 + 256) if ncol == 256 else slice(i1 * 128, i1 * 128 + 128)
                        diag = (g == p)
                        S.op(PE, lambda e, sb_=sb_, j=j, coff=coff, ncol=ncol, qsl=qsl, Kx=Kx, Qx=Qx, diag=diag: e.matmul(
                            PS[sb_][:, coff:coff + ncol], lhsT=Kx[:, j * 128:(j + 1) * 128], rhs=Qx[:, qsl],
                            start=True, stop=(not diag)), reads=[R_Kx, R_Qx], writes=[R_PS[sb_]])
                        if diag:
                            S.op(PE, lambda e, sb_=sb_, coff=coff: e.matmul(
                                PS[sb_][:, coff:coff + 128], lhsT=identb, rhs=negmask, start=False, stop=True),
                                reads=[R_const], writes=[R_PS[sb_]])
                        allblocks.append((j, gi * 512 + coff, ncol))
                    E, R_E = e_s.next()
                    S.op(ACT, lambda e, E=E, base=base, ntot=ntot: e.activation(
                        out=E[:, 0:ntot], in_=PSALL[:, base * 512:base * 512 + ntot], func=AF.Exp),
                        reads=[R_PS[base + gi] for gi in range(len(gs))], writes=[R_E])
                    ust[u] = (E, R_E, allblocks)

                def emit_PV(u, V1=V1, R_V=R_V, QA=QA, R_QA=R_QA):
                    p, m, gs = u
                    g = gs[-1]
                    E, R_E, blocks = ust.pop(u)
                    obm = 6 + m
                    if gs[0] == 0:
                        S.op(PE, lambda e, obm=obm: e.matmul(PS[obm][:, 0:512], lhsT=zerosb, rhs=QA[:, 0:512],
                                                            start=True, stop=False, skip_group_check=True),
                             reads=[R_const, R_QA], writes=[R_PS[obm]])
                    for (j, coff, ncol) in sorted(blocks):
                        tiles = [(0, coff), (1, coff + 128)] if ncol == 256 else [(1, coff)]
                        for (t, c0) in tiles:
                            last = 2 * p + t
                            S.op(PE, lambda e, obm=obm, t=t, c0=c0, j=j, E=E, last=last: e.matmul(
                                PS[obm][:, t * 256:t * 256 + 129], lhsT=E[:, c0:c0 + 128], rhs=V1[:, j, 0:129],
                                start=False, stop=(j == last), skip_group_check=True),
                                reads=[R_E, R_V], writes=[R_PS[obm]])
                    if g == p:
                        if p not in oraw_of:
                            oraw_of[p] = oraw_s.next()
                        opair, R_opair = oraw_of[p]
                        S.op(DVE, lambda e, opair=opair, obm=obm, m=m: e.tensor_copy(
                            out=opair[:, :, m, 0:129],
                            in_=PS[obm][:, :].rearrange("p (t n) -> p t n", t=2)[:, :, 0:129]),
                            reads=[R_PS[obm]], writes=[R_opair])
                        if m == 1:
                            for t in range(2):
                                i = 2 * p + t
                                oraw, R_oraw = opair[:, t], R_opair
                                sm, R_sm = sm_s.next()
                                t1, R_t1 = t1_s.next()
                                S.op(DVE, lambda e, sm=sm, oraw=oraw: e.reciprocal(out=sm[:, 0:2], in_=oraw[:, :, 128]),
                                     reads=[R_oraw], writes=[R_sm])
                                S.op(DVE, lambda e, sm=sm: e.tensor_scalar(out=sm[:, 2:3], in0=sm[:, 1:2], scalar1=neglam,
                                                                           scalar2=None, op0=ALU.mult),
                                     reads=[R_sm, R_misc], writes=[R_sm])
                                S.op(DVE, lambda e, sm=sm, t1=t1, oraw=oraw: e.tensor_scalar(
                                    out=t1, in0=oraw[:, 1, 0:128], scalar1=sm[:, 2:3], scalar2=None, op0=ALU.mult),
                                    reads=[R_sm, R_oraw], writes=[R_t1])
                                S.op(DVE, lambda e, sm=sm, t1=t1, oraw=oraw, i=i: e.scalar_tensor_tensor(
                                    out=o_all[:, i, :], in0=oraw[:, 0, 0:128], scalar=sm[:, 0:1], in1=t1,
                                    op0=ALU.mult, op1=ALU.add), reads=[R_sm, R_t1, R_oraw], writes=[R_oall])
                                S.op(DVE, lambda e, i=i: e.scalar_tensor_tensor(
                                    out=junk, in0=o_all[:, i, :], scalar=1.0, in1=o_all[:, i, :], op0=ALU.mult,
                                    op1=ALU.mult, accum_out=ssq[:, i:i + 1]), reads=[R_oall], writes=[R_junk, R_ssq])

                emit_S(units[0])
                emit_S(units[1])
                for n_ in range(len(units)):
                    if n_ + 2 < len(units):
                        emit_S(units[n_ + 2])
                    emit_PV(units[n_])
                    if n_ == 22:
                        if pending_T:
                            pending_T.pop(0)()
                        if h + 1 < 8:
                            emit_proj(h + 1)
                S.op(ACT, lambda e: e.activation(out=ssq, in_=ssq, func=AF.Sqrt, scale=1.0 / 128, bias=epsc),
                     reads=[R_ssq, R_misc], writes=[R_ssq])
                S.op(DVE, lambda e: e.reciprocal(out=ssq, in_=ssq), reads=[R_ssq], writes=[R_ssq])
                S.op(DVE, lambda e: e.tensor_tensor(out=o_all, in0=o_all,
                                                    in1=ssq.unsqueeze(2).broadcast_to([128, 16, 128]), op=ALU.mult),
                     reads=[R_oall, R_ssq], writes=[R_oall])
                S.op(DVE, lambda e: e.tensor_tensor(out=on_all, in0=o_all,
                                                    in1=hg.unsqueeze(1).broadcast_to([128, 16, 128]), op=ALU.mult),
                     reads=[R_oall, R_pv], writes=[R_on])

                def do_T(h=h):
                    for half in range(2):
                        tb_ = nbank(0, 6)
                        psb = PS[tb_][:, :].bitcast(BF16)
                        for t in range(8):
                            tl = half * 8 + t
                            S.op(PE, lambda e, t=t, tl=tl, psb=psb: e.transpose(out=psb[:, t * 128:(t + 1) * 128],
                                                                             in_=on_all[:, tl, :], identity=identb),
                                 reads=[R_on, R_const], writes=[R_PS[tb_]])
                        copy_op(evac_eng(), O1[:, h, half * 1024:(half + 1) * 1024], psb, [R_PS[tb_]],
                                [R_o1[half * 2], R_o1[half * 2 + 1]])
                pending_T.append(do_T)
            while pending_T:
                pending_T.pop(0)()
            S.barrier()
            if upto == "da":
                return O1

            def merge_branch(bi, wout_d, Osrc, R_src, Mb, R_mb):
                wout_v = wout_d.rearrange("(c p) n -> p c n", p=128)
                sig_s = Slots([(A.f32(512), Res(f"sig{bi}_{i}")) for i in range(2)])
                t_s = Slots([(A.f32(512), Res(f"tt{bi}_{i}")) for i in range(2)])
                wo_s = Slots([(A.bf16(DC * 128).rearrange("p (c n) -> p c n", c=DC), Res(f"wo{bi}_{i}"),
                               S.new_dma_sem(f"d_wo{bi}_{i}")) for i in range(2)])
                wg_s = Slots([(A.bf16(DC * 128).rearrange("p (c n) -> p c n", c=DC), Res(f"wgt{bi}_{i}"),
                               S.new_dma_sem(f"d_wgt{bi}_{i}")) for i in range(2)])
                for dc in range(DC):
                    wo, R_wo, s1 = wo_s.next()
                    wgt, R_wgt, s2 = wg_s.next()
                    wload(wo, wout_v[:, :, dc * 128:(dc + 1) * 128], s1, R_wo)
                    wload(wgt, w_bg_v[:, :, bi * D + dc * 128:bi * D + (dc + 1) * 128], s2, R_wgt)
                    for g in range(4):
                        tok = slice(g * 512, (g + 1) * 512)
                        by, bgt = nbank(), nbank()
                        for c in range(DC):
                            S.op(PE, lambda e, by=by, c=c, tok=tok, wo=wo: e.matmul(
                                PS[by][:, :], lhsT=wo[:, c, :], rhs=Osrc[:, c, tok],
                                start=(c == 0), stop=(c == DC - 1)), reads=[R_wo, R_src[g]], writes=[R_PS[by]])
                        for c in range(DC):
                            S.op(PE, lambda e, bgt=bgt, c=c, tok=tok, wgt=wgt: e.matmul(
                                PS[bgt][:, :], lhsT=wgt[:, c, :], rhs=hT[:, c, tok],
                                start=(c == 0), stop=(c == DC - 1)), reads=[R_wgt, R_h[g]], writes=[R_PS[bgt]])
                        sg, R_sg = sig_s.next()
                        S.op(ACT, lambda e, sg=sg, bgt=bgt, dc=dc: e.activation(
                            out=sg, in_=PS[bgt][:, :], func=AF.Sigmoid, bias=pcol("bbg", bi * 8 + dc)),
                            reads=[R_PS[bgt], R_pv], writes=[R_sg])
                        if bi == 0:
                            S.op(DVE, lambda e, sg=sg, by=by, dc=dc, tok=tok: e.tensor_tensor(
                                out=Mb[:, dc, tok], in0=sg, in1=PS[by][:, :], op=ALU.mult),
                                reads=[R_sg, R_PS[by]], writes=[R_mb[g]])
                        else:
                            tt, R_tt = t_s.next()
                            S.op(DVE, lambda e, sg=sg, by=by, tt=tt: e.tensor_tensor(
                                out=tt, in0=sg, in1=PS[by][:, :], op=ALU.mult),
                                reads=[R_sg, R_PS[by]], writes=[R_tt])
                            S.op(DVE, lambda e, tt=tt, dc=dc, tok=tok: e.tensor_tensor(
                                out=Mb[:, dc, tok], in0=Mb[:, dc, tok], in1=tt, op=ALU.add),
                                reads=[R_tt, R_mb[g]], writes=[R_mb[g]])
                S.barrier()

            A.off = ZBASE
            M_, R_m = O2, R_o2
            merge_branch(0, Wd_["w_da_out"], O1, R_o1, M_, R_m)
            O2, R_o2 = O1, R_o1
            if upto == "m0":
                return M_

            A.off = ZBASE
            L = 1024
            xl = A.f32(4 + L); R_xl = Res("xl")
            xc_s = [(A.f32(L), Res(f"xc{i}")) for i in range(2)]
            tr_s = [(A.f32(L), Res(f"tr{i}")) for i in range(2)]
            ti_s = [(A.f32(L), Res(f"ti{i}")) for i in range(2)]
            sb2 = A.f32(L); R_sb2 = Res("sb2")
            yg_s = [(A.bf16(L), Res(f"yg{i}")) for i in range(2)]
            hlast = A.f32(2); R_hl = Res("hlast")
            wxy_s = Slots([(A.bf16(DC * 128).rearrange("p (c n) -> p c n", c=DC), Res(f"wxy{i}"),
                            S.new_dma_sem(f"d_wxy{i}")) for i in range(3)])
            wax_s = Slots([(A.f32(256), Res(f"wax{i}"), S.new_dma_sem(f"d_wax{i}")) for i in range(2)])
            wcur = {}

            def lru_round(nb_, nf_):
                B = nf_ is not None
                A_ = nb_ is not None
                if B:
                    c, half = divmod(nf_, 2)
                    t0 = half * L
                    if half == 0:
                        wx, R_wx, s_wx = wxy_s.next()
                        wy, R_wy, s_wy = wxy_s.next()
                        wax, R_wax, s_wax = wax_s.next()
                        wload(wx, w_in_v[:, :, 3072 + c * 128:3072 + (c + 1) * 128], s_wx, R_wx)
                        wload(wy, w_in_v[:, :, 4096 + c * 128:4096 + (c + 1) * 128], s_wy, R_wy)
                        S.dma(SP, [lambda e, c=c, wax=wax: e.dma_start(out=wax[:, 0:128],
                                                                       in_=Wd_["lru_wa"][c * 128:(c + 1) * 128, :]),
                                   lambda e, c=c, wax=wax: e.dma_start(out=wax[:, 128:256],
                                                                       in_=Wd_["lru_wx"][c * 128:(c + 1) * 128, :])],
                              s_wax, writes=[R_wax])
                        wcur[c] = (wx, R_wx, wy, R_wy, wax, R_wax)
                    wx, R_wx, wy, R_wy, wax, R_wax = wcur[c]
                    xc, R_xc = xc_s[nf_ % 2]
                    tr, R_tr = tr_s[nf_ % 2]
                    ti, R_ti = ti_s[nf_ % 2]
                    yg, R_yg = yg_s[nf_ % 2]
                    ybanks = (4, 5) if nf_ % 2 == 0 else (6, 7)
                    if half == 0:
                        S.op(DVE, lambda e: e.memset(xl[:, 0:4], 0.0), writes=[R_xl])
                    else:
                        S.op(DVE, lambda e: e.tensor_copy(out=xl[:, 1:4], in_=xl[:, L + 1:L + 4]),
                             reads=[R_xl], writes=[R_xl])
                    for g in range(2):
                        tok = slice(t0 + g * 512, t0 + (g + 1) * 512)
                        gq = half * 2 + g
                        bx_ = nbank(0, 4)
                        for k in range(DC):
                            S.op(PE, lambda e, bx_=bx_, k=k, tok=tok, wx=wx: e.matmul(
                                PS[bx_][:, :], lhsT=wx[:, k, :], rhs=hT[:, k, tok], start=(k == 0), stop=(k == DC - 1)),
                                reads=[R_wx, R_h[gq]], writes=[R_PS[bx_]])
                        S.op(ACT, lambda e, bx_=bx_, g=g: e.activation(out=xl[:, 4 + g * 512:4 + (g + 1) * 512],
                                                                       in_=PS[bx_][:, :], func=AF.Copy),
                             reads=[R_PS[bx_]], writes=[R_xl])
                    for g in range(2):
                        tok = slice(t0 + g * 512, t0 + (g + 1) * 512)
                        gq = half * 2 + g
                        by_ = ybanks[g]
                        for k in range(DC):
                            S.op(PE, lambda e, by_=by_, k=k, tok=tok, wy=wy: e.matmul(
                                PS[by_][:, :], lhsT=wy[:, k, :], rhs=hT[:, k, tok], start=(k == 0), stop=(k == DC - 1)),
                                reads=[R_wy, R_h[gq]], writes=[R_PS[by_]])
                if A_:
                    ca, halfa = divmod(nb_, 2)
                    ta0 = halfa * L
                    xca, R_xca = xc_s[nb_ % 2]
                    tra, R_tra = tr_s[nb_ % 2]
                    tia, R_tia = ti_s[nb_ % 2]
                    yga, R_yga = yg_s[nb_ % 2]
                    S.op(DVE, lambda e: e.scalar_tensor_tensor(out=tia, in0=tia, scalar=1.0, in1=xca,
                                                               op0=ALU.add, op1=ALU.mult),
                         reads=[R_tia, R_xca], writes=[R_tia])
                    S.op(ACT, lambda e: e.activation(out=sb2, in_=tra, func=AF.Exp, scale=pcol("coef", ca),
                                                     bias=pcol("coef", ca)),
                         reads=[R_tra, R_pv], writes=[R_sb2])
                    S.op(ACT, lambda e: e.activation(out=tra, in_=tra, func=AF.Exp, scale=pcol("hcoef", ca),
                                                     bias=pcol("hcoef", ca)),
                         reads=[R_tra, R_pv], writes=[R_tra])
                if B:
                    S.op(DVE, lambda e: e.tensor_scalar(out=xc, in0=xl[:, 4:4 + L], scalar1=pcol("convw", 24 + c),
                                                        scalar2=pcol("convb", c), op0=ALU.mult, op1=ALU.add),
                         reads=[R_xl, R_pv], writes=[R_xc])
                    for tap in (2, 1, 0):
                        sh = 3 - tap
                        S.op(DVE, lambda e, tap=tap, sh=sh: e.scalar_tensor_tensor(
                            out=xc, in0=xl[:, 4 - sh:4 - sh + L], scalar=pcol("convw", tap * 8 + c), in1=xc,
                            op0=ALU.mult, op1=ALU.add), reads=[R_xl, R_pv, R_xc], writes=[R_xc])
                if A_:
                    S.op(ACT, lambda e: e.activation(out=sb2, in_=sb2, func=AF.Sqrt, scale=-0.25, bias=qtr),
                         reads=[R_sb2, R_misc], writes=[R_sb2])
                if B:
                    rib = []
                    for g in range(2):
                        loc = slice(g * 512, (g + 1) * 512)
                        br_, bi_ = nbank(0, 4), nbank(0, 4)
                        S.op(PE, lambda e, br_=br_, loc=loc: e.matmul(
                            PS[br_][:, :], lhsT=wax[:, 0:128], rhs=xc[:, loc], start=True, stop=True),
                            reads=[R_wax, R_xc], writes=[R_PS[br_]])
                        S.op(PE, lambda e, bi_=bi_, loc=loc: e.matmul(
                            PS[bi_][:, :], lhsT=wax[:, 128:256], rhs=xc[:, loc], start=True, stop=True),
                            reads=[R_wax, R_xc], writes=[R_PS[bi_]])
                        rib.append((br_, bi_, loc))
                if A_:
                    S.op(DVE, lambda e: e.tensor_tensor(out=tia, in0=tia, in1=sb2, op=ALU.mult),
                         reads=[R_tia, R_sb2], writes=[R_tia])
                    if halfa == 0:
                        S.op(DVE, lambda e: e.tensor_tensor_scan(out=tia, data0=tra, data1=tia, initial=0.0,
                                                                 op0=ALU.mult, op1=ALU.add),
                             reads=[R_tra, R_tia], writes=[R_tia])
                    else:
                        S.op(DVE, lambda e: e.tensor_tensor_scan(out=tia, data0=tra, data1=tia, initial=hlast[:, 0:1],
                                                                 op0=ALU.mult, op1=ALU.add),
                             reads=[R_tra, R_tia, R_hl], writes=[R_tia])
                    S.op(DVE, lambda e: e.tensor_copy(out=hlast[:, 0:1], in_=tia[:, L - 1:L]),
                         reads=[R_tia], writes=[R_hl])
                    S.op(DVE, lambda e: e.tensor_tensor(out=O2[:, ca, ta0:ta0 + L], in0=tia, in1=yga, op=ALU.mult),
                         reads=[R_tia, R_yga], writes=[R_o2[halfa * 2], R_o2[halfa * 2 + 1]])
                if B:
                    for (br_, bi_, loc) in rib:
                        S.op(ACT, lambda e, br_=br_, loc=loc: e.activation(
                            out=tr[:, loc], in_=PS[br_][:, :], func=AF.Tanh, scale=0.5, bias=pcol("hba", c)),
                            reads=[R_PS[br_], R_pv], writes=[R_tr])
                        S.op(ACT, lambda e, bi_=bi_, loc=loc: e.activation(
                            out=ti[:, loc], in_=PS[bi_][:, :], func=AF.Tanh, scale=0.5, bias=pcol("hbx", c)),
                            reads=[R_PS[bi_], R_pv], writes=[R_ti])
                    for g in range(2):
                        loc = slice(g * 512, (g + 1) * 512)
                        S.op(ACT, lambda e, g=g, loc=loc: e.activation(
                            out=yg[:, loc], in_=PS[ybanks[g]][:, :], func=AF.Gelu_apprx_tanh),
                            reads=[R_PS[ybanks[g]]], writes=[R_yg])

            lru_round(None, 0)
            for n in range(16):
                lru_round(n, n + 1 if n + 1 < 16 else None)
            S.barrier()
            if upto == "lru":
                return O2
            A.off = ZBASE
            merge_branch(1, Wd_["w_lru_out"], O2, R_o2, M_, R_m)
            if upto == "m1":
                return M_

            A.off = ZBASE
            memT = A.bf16(DC * NMEM).rearrange("p (c t) -> p c t", c=DC); R_memT = Res("memT")
            ZM = A.off
            mstg = A.f32(D); R_mstg = Res("mstg"); s_mstg = S.new_dma_sem("d_mstg")
            memTf = A.f32(DC * NMEM).rearrange("p (c t) -> p c t", c=DC); R_memTf = Res("memTf")
            sq_s = Slots([(A.bf16(512), Res(f"m3sq{i}")) for i in range(2)])
            rsm = A.f32(NMEM); R_rsm = Res("rsm")
            for i in range(2):
                S.dma(SP, [lambda e, i=i: e.dma_start(out=mstg, in_=mem_d[i * 128:(i + 1) * 128, :])], s_mstg,
                      writes=[R_mstg])
                for hb_ in range(2):
                    b = nbank()
                    for j in range(4):
                        c = hb_ * 4 + j
                        S.op(PE, lambda e, b=b, j=j, c=c: e.transpose(out=PS[b][:, j * 128:(j + 1) * 128],
                                                                    in_=mstg[:, c * 128:(c + 1) * 128], identity=identf),
                             reads=[R_mstg, R_const], writes=[R_PS[b]])
                    copy_op(evac_eng(), memTf[:, hb_ * 4:(hb_ + 1) * 4, i * 128:(i + 1) * 128],
                            PS[b][:, :].rearrange("p (c t) -> p c t", c=4), [R_PS[b]], [R_memTf])
            rms_rstd([memTf[:, c, :] for c in range(DC)], [R_memTf], NMEM, sq_s, rsm, R_rsm, 6)
            for c in range(DC):
                S.op(DVE, lambda e, c=c: e.scalar_tensor_tensor(out=memT[:, c, :], in0=memTf[:, c, :],
                                                                scalar=pcol("g_mem", c), in1=rsm,
                                                                op0=ALU.mult, op1=ALU.mult),
                     reads=[R_memTf, R_pv, R_rsm], writes=[R_memT])
            S.barrier()
            A.off = ZM
            KmT = A.bf16(DC * NMEM).rearrange("p (c t) -> p c t", c=DC); R_KmT = Res("KmT")
            Vm = A.bf16(2 * D).rearrange("p (m n) -> p m n", m=2); R_Vm = Res("Vm")
            wkv_s = Slots([(A.bf16(DC * 256).rearrange("p (c n) -> p c n", c=DC), Res(f"wkv{i}"),
                            S.new_dma_sem(f"d_wkv{i}")) for i in range(4)])
            qc_s = Slots([(A.bf16(2 * 512).rearrange("p (d t) -> p d t", d=2), Res(f"qc{i}")) for i in range(3)])
            et_s = Slots([(A.bf16(2 * 512).rearrange("p (m t) -> p m t", m=2), Res(f"et{i}")) for i in range(2)])
            rd_s = Slots([(A.f32(512), Res(f"rd{i}")) for i in range(2)])
            w_kv_v = Wd_["w_mem_kv"].rearrange("(c p) n -> p c n", p=128)
            for cg in range(4):
                wkv, R_wkv, s_wkv = wkv_s.next()
                wload(wkv, w_kv_v[:, :, cg * 256:(cg + 1) * 256], s_wkv, R_wkv)
                bb = (nbank(), nbank())
                for k in range(DC):
                    for cc in range(2):
                        S.op(PE, lambda e, b=bb[cc], k=k, cc=cc, wkv=wkv: e.matmul(
                            PS[b][:, 0:NMEM], lhsT=wkv[:, k, cc * 128:(cc + 1) * 128], rhs=memT[:, k, :],
                            start=(k == 0), stop=(k == DC - 1)), reads=[R_wkv, R_memT], writes=[R_PS[bb[cc]]])
                for cc in range(2):
                    copy_op(evac_eng(), KmT[:, cg * 2 + cc, :], PS[bb[cc]][:, 0:NMEM], [R_PS[bb[cc]]], [R_KmT],
                            scale=1.0 / 16)
            for cg in range(4):
                wkv, R_wkv, s_wkv = wkv_s.next()
                wload(wkv, w_kv_v[:, :, D + cg * 256:D + (cg + 1) * 256], s_wkv, R_wkv)
                bb = (nbank(), nbank())
                for k in range(DC):
                    for mc in range(2):
                        S.op(PE, lambda e, b=bb[mc], k=k, mc=mc, wkv=wkv: e.matmul(
                            PS[b][:, 0:256], lhsT=memT[:, k, mc * 128:(mc + 1) * 128], rhs=wkv[:, k, :],
                            start=(k == 0), stop=(k == DC - 1)), reads=[R_wkv, R_memT], writes=[R_PS[bb[mc]]])
                for mc in range(2):
                    copy_op(evac_eng(), Vm[:, mc, cg * 256:(cg + 1) * 256], PS[bb[mc]][:, 0:256], [R_PS[bb[mc]]], [R_Vm])
            cst = {}
            wq_of = {}

            def ca_P(u):
                hh, g = u
                if g == 0:
                    wqc, R_wqc, s_wqc = wkv_s.next()
                    wload(wqc, w_in_v[:, :, 5120 + hh * 256:5120 + (hh + 1) * 256], s_wqc, R_wqc)
                    wq_of[hh] = (wqc, R_wqc)
                wqc, R_wqc = wq_of[hh]
                tok = slice(g * 512, (g + 1) * 512)
                qc, R_qc = qc_s.next()
                for dd in range(2):
                    b = nbank()
                    for k in range(DC):
                        S.op(PE, lambda e, b=b, k=k, dd=dd, tok=tok, wqc=wqc: e.matmul(
                            PS[b][:, :], lhsT=wqc[:, k, dd * 128:(dd + 1) * 128], rhs=hT[:, k, tok],
                            start=(k == 0), stop=(k == DC - 1)), reads=[R_wqc, R_h[g]], writes=[R_PS[b]])
                    copy_op(DVE if dd else ACT, qc[:, dd, :], PS[b][:, :], [R_PS[b]], [R_qc])
                cst[u] = [qc, R_qc]

            def ca_S(u):
                hh, g = u
                qc, R_qc = cst[u]
                et, R_et = et_s.next()
                for mc in range(2):
                    b = nbank()
                    for dd in range(2):
                        S.op(PE, lambda e, b=b, dd=dd, mc=mc, hh=hh, qc=qc: e.matmul(
                            PS[b][:, :], lhsT=KmT[:, hh * 2 + dd, mc * 128:(mc + 1) * 128], rhs=qc[:, dd, :],
                            start=(dd == 0), stop=(dd == 1)), reads=[R_KmT, R_qc], writes=[R_PS[b]])
                    S.op(ACT, lambda e, b=b, mc=mc, et=et: e.activation(out=et[:, mc, :], in_=PS[b][:, :], func=AF.Exp),
                         reads=[R_PS[b]], writes=[R_et])
                cst[u] = [et, R_et]

            def ca_V(u):
                hh, g = u
                tok = slice(g * 512, (g + 1) * 512)
                et, R_et = cst.pop(u)
                bd = nbank()
                for mc in range(2):
                    S.op(PE, lambda e, bd=bd, mc=mc, et=et: e.matmul(PS[bd][:, :], lhsT=onesb, rhs=et[:, mc, :],
                                                                   start=(mc == 0), stop=(mc == 1)),
                         reads=[R_et, R_const], writes=[R_PS[bd]])
                rd, R_rd = rd_s.next()
                S.op(DVE, lambda e, bd=bd, rd=rd: e.reciprocal(out=rd, in_=PS[bd][:, :]), reads=[R_PS[bd]],
                     writes=[R_rd])
                for dv in range(2):
                    b = nbank()
                    for mc in range(2):
                        S.op(PE, lambda e, b=b, mc=mc, dv=dv, hh=hh, et=et: e.matmul(
                            PS[b][:, :], lhsT=Vm[:, mc, hh * 256 + dv * 128:hh * 256 + (dv + 1) * 128],
                            rhs=et[:, mc, :], start=(mc == 0), stop=(mc == 1)),
                            reads=[R_Vm, R_et], writes=[R_PS[b]])
                    S.op(DVE, lambda e, b=b, rd=rd, hh=hh, dv=dv, tok=tok: e.tensor_tensor(
                        out=O2[:, hh * 2 + dv, tok], in0=PS[b][:, :], in1=rd, op=ALU.mult),
                        reads=[R_PS[b], R_rd], writes=[R_o2[g]])

            cunits = [(hh, g) for hh in range(4) for g in range(4)]
            ca_P(cunits[0])
            ca_P(cunits[1])
            ca_S(cunits[0])
            for n_ in range(16):
                if n_ + 2 < 16:
                    ca_P(cunits[n_ + 2])
                if n_ + 1 < 16:
                    ca_S(cunits[n_ + 1])
                ca_V(cunits[n_])
            S.barrier()
            if upto == "ca":
                return O2
            A.off = ARENA_WORDS - DC * D // 2
            wm = A.bf16(DC * D).rearrange("p (c n) -> p c n", c=DC); R_wm = Res("wm"); s_wm = S.new_dma_sem("d_wm")
            wload(wm, Wd_["w_mix"].rearrange("(c p) n -> p c n", p=128), s_wm, R_wm)
            A.off = ZBASE
            merge_branch(2, Wd_["w_ca_out"], O2, R_o2, M_, R_m)
            if upto == "m2":
                return M_

            A.off = O1BASE
            f_s = Slots([(A.f32(DC * 512).rearrange("p (c t) -> p c t", c=DC), Res(f"m4f{i}"), A.f32(512),
                          Res(f"rs4_{i}")) for i in range(1)])
            YTOP = ZBASE
            f_s2 = (arena_t[:, YTOP:YTOP + DC * 512].rearrange("p (c t) -> p c t", c=DC), Res("m4f1"),
                    arena_t[:, YTOP + DC * 512:YTOP + DC * 512 + 512], Res("rs4_1"))
            f_list = [f_s.items[0], f_s2]
            sq_s = Slots([(A.bf16(512), Res(f"m4sq{i}")) for i in range(4)])
            lag = []
            for g in range(4):
                tok = slice(g * 512, (g + 1) * 512)
                fT, R_f, rs4, R_rs4 = f_list[g % 2]
                sb_ = 6 + g % 2
                for dc in range(DC):
                    b = nbank()
                    while len(lag) > 1:
                        lag.pop(0)()
                    for c in range(DC):
                        S.op(PE, lambda e, b=b, c=c, dc=dc, tok=tok: e.matmul(
                            PS[b][:, :], lhsT=wm[:, c, dc * 128:(dc + 1) * 128], rhs=M_[:, c, tok],
                            start=(c == 0), stop=(c == DC - 1)), reads=[R_wm, R_m[g]], writes=[R_PS[b]])
                    sq, R_sq = sq_s.next()
                    S.op(ACT, lambda e, sq=sq, b=b: e.activation(out=sq, in_=PS[b][:, :], func=AF.Square),
                         reads=[R_PS[b]], writes=[R_sq])
                    S.op(ACT, lambda e, b=b, dc=dc, fT=fT: e.activation(out=fT[:, dc, :], in_=PS[b][:, :], func=AF.Copy),
                         reads=[R_PS[b]], writes=[R_f])
                    lag.append(lambda sq=sq, R_sq=R_sq, dc=dc, sb_=sb_: S.op(PE, lambda e: e.matmul(
                        PS[sb_][:, :], lhsT=onesb, rhs=sq, start=(dc == 0), stop=(dc == DC - 1)),
                        reads=[R_sq, R_const], writes=[R_PS[sb_]]))
                    S.drain(3)
                while lag:
                    lag.pop(0)()
                S.defer(lambda rs4=rs4, R_rs4=R_rs4, sb_=sb_: S.op(ACT, lambda e: e.activation(
                    out=rs4, in_=PS[sb_][:, :], func=AF.Sqrt, scale=1.0 / D, bias=epsc),
                    reads=[R_PS[sb_], R_misc], writes=[R_rs4]))
                S.defer(lambda rs4=rs4, R_rs4=R_rs4: S.op(DVE, lambda e: e.reciprocal(out=rs4, in_=rs4),
                                                          reads=[R_rs4], writes=[R_rs4]))
                for dc in range(DC):
                    S.defer(lambda dc=dc, fT=fT, rs4=rs4, R_f=R_f, R_rs4=R_rs4: S.op(DVE, lambda e: e.tensor_tensor(
                        out=fT[:, dc, :], in0=fT[:, dc, :], in1=rs4, op=ALU.mult), reads=[R_f, R_rs4], writes=[R_f]))
                    S.defer(lambda dc=dc, fT=fT, tok=tok, R_f=R_f, g=g: S.op(DVE, lambda e: e.scalar_tensor_tensor(
                        out=xT[:, dc, tok], in0=fT[:, dc, :], scalar=pcol("g_mixpost", dc), in1=xT[:, dc, tok],
                        op0=ALU.mult, op1=ALU.add), reads=[R_f, R_pv, R_x[g]], writes=[R_x[g]]))
            S.drain()
            S.barrier()
            return None

        stages = os.environ.get("MK_STAGES", "all") if dbg is None else dbg
        dump = None
        if stages != "x":
            ffn_stage("g_f1pre", "pg_f1", Wd_["f1_wg"], Wd_["f1_wu"], Wd_["f1_wd"], "f1")
        if stages not in ("x", "f1"):
            dump = mixer_stage(None if stages in ("all", "mix") else stages)
        if stages == "all":
            ffn_stage("g_f2pre", "pg_f2", Wd_["f2_wg"], Wd_["f2_wu"], Wd_["f2_wd"], "f2", last=True)
        if dump is not None:
            for g in range(4):
                tok = slice(g * 512, (g + 1) * 512)
                S.op(DVE, lambda e, tok=tok: e.tensor_copy(out=xT[:, :, tok], in_=dump[:, :, tok]), writes=[R_x[g]])
            S.barrier()

        if stages == "all":
            A.off = PERSIST
        else:
            A.off = PERSIST
        ostg = Slots([(A.f32(D), Res(f"ostg{i}"), S.new_dma_sem(f"d_ostg{i}")) for i in range(4)])
        out_sems = []
        for i in range(16):
            buf, R_b, sem = ostg.next()
            if sem not in out_sems:
                out_sems.append(sem)
            for hb in range(2):
                b = nbank()
                for j in range(4):
                    c = hb * 4 + j
                    S.op(PE, lambda e, b=b, j=j, c=c, i=i: e.transpose(
                        out=PS[b][:, j * 128:(j + 1) * 128], in_=xT[:, c, i * 128:(i + 1) * 128], identity=identf),
                        reads=[R_x[i // 4], R_const], writes=[R_PS[b]])
                copy_op(evac_eng(), buf[:, hb * 512:(hb + 1) * 512], PS[b][:, :], [R_PS[b]], [R_b])
            S.dma(SP if i % 2 == 0 else POOL,
                  [lambda e, buf=buf, i=i: e.dma_start(out=out_d[i * 128:(i + 1) * 128, :], in_=buf)], sem,
                  reads=[R_b])
        for sem in out_sems:
            SP.prog.append(lambda e, sem=sem, v=S.dma_val[sem]: e.wait_ge(sem, v))
        S.emit()
    return nc


def _host_constants():
    bf = ml_dtypes.bfloat16
    identf = np.eye(128, dtype=np.float32)
    kk = np.arange(128)[:, None]
    qq = np.arange(128)[None, :]
    negmask = np.where(qq >= kk, 0.0, -30000.0).astype(np.float32)
    c_bf = np.concatenate([identf, np.ones((128, 128), np.float32), negmask,
                           np.zeros((128, 128), np.float32)], axis=1).astype(bf)
    pos = np.arange(T)
    hi = (pos // 128) * 128
    lo = pos % 128
    qaug = np.stack([hi, lo, np.ones(T), np.ones(T)]).astype(np.float32).astype(bf)
    kaug = []
    for h in range(8):
        s = 2.0 ** (-(h + 1))
        kaug.append(np.stack([-s * np.ones(T), -s * np.ones(T), s * hi, s * lo]))
    kaug = np.concatenate(kaug, 0).astype(np.float32).astype(bf)
    return identf, c_bf, qaug, kaug


def _pack_inputs(inp):
    def col(v):
        return np.ascontiguousarray(np.asarray(v, np.float32).reshape(-1, 128).T)
    cols = [col(inp["ffn1_pre_g"][0]), col(inp["ffn1_post_g"][0]), col(inp["mix_pre_g"][0]),
            col(inp["mix_post_g"][0]), col(inp["ffn2_pre_g"][0]), col(inp["ffn2_post_g"][0]),
            col(inp["mem_g"][0])]
    for t in range(4):
        cols.append(col(inp["lru_conv_w"][0, t]))
    cols += [col(inp["lru_conv_b"][0]), col(inp["lru_b_a"][0]), col(inp["lru_b_x"][0]), col(inp["lru_lambda"][0])]
    cols.append(col(inp["b_branch_gate"][0]))
    pvec = np.ascontiguousarray(np.concatenate(cols, axis=1))
    assert pvec.shape == (128, NV_IN), pvec.shape
    lamv = np.concatenate([inp["da_lambda_q1"][0], inp["da_lambda_k1"][0], inp["da_lambda_q2"][0],
                           inp["da_lambda_k2"][0]]).astype(np.float32)[None, :]
    identf, c_bf, qaug, kaug = _host_constants()
    f32 = lambda a: np.ascontiguousarray(np.asarray(a, np.float32))
    shared = {
        "f1_wg": f32(inp["ffn1_w_gate"][0]), "f1_wu": f32(inp["ffn1_w_up"][0]), "f1_wd": f32(inp["ffn1_w_down"][0]),
        "f2_wg": f32(inp["ffn2_w_gate"][0]), "f2_wu": f32(inp["ffn2_w_up"][0]), "f2_wd": f32(inp["ffn2_w_down"][0]),
        "w_in": f32(inp["w_in"][0]), "w_da_out": f32(inp["w_da_out"][0]), "w_lru_out": f32(inp["w_lru_out"][0]),
        "w_ca_out": f32(inp["w_ca_out"][0]), "w_bg": f32(inp["w_branch_gate"][0]), "w_mix": f32(inp["w_mix_out"][0]),
        "w_mem_kv": f32(inp["w_mem_kv"][0]),
        "lru_wa": f32(np.asarray(inp["lru_w_a"][0]).reshape(D, 128)),
        "lru_wx": f32(np.asarray(inp["lru_w_x"][0]).reshape(D, 128)),
        "pvec": pvec, "lamv": lamv, "headg": f32(inp["da_head_g"]),
        "c_identf": identf, "c_bf": c_bf, "qaug": qaug, "kaug": kaug,
    }
    return shared


_NC_CACHE = {}


def kernel(**inputs):
    inputs = {k: np.asarray(v) for k, v in inputs.items()}
    dbg = os.environ.get("MK_STAGES", "all")
    if dbg not in _NC_CACHE:
        _NC_CACHE[dbg] = build_program(dbg)
    nc = _NC_CACHE[dbg]
    shared = _pack_inputs(inputs)
    x = np.ascontiguousarray(inputs["x"], dtype=np.float32)
    mem = np.ascontiguousarray(inputs["mem"], dtype=np.float32)
    in_maps = []
    for b in range(NCORES):
        m = dict(shared)
        m["x"] = x[b]
        m["mem"] = mem[b]
        in_maps.append(m)
    res = run_bass_kernel_spmd(nc, in_maps, core_ids=list(range(NCORES)))
    out = np.stack([np.asarray(r["out"], dtype=np.float32) for r in res.results], axis=0)
    return out
```
